# Optimizing a Trainium2 kernel written in Bass

```python
import jax, jax.numpy as jnp
from jax import lax
import numpy as np

D_MODEL = 1024
BATCH = 2
SEQ = 8192
DEPTH = 1

D_MIX = D_MODEL
D_LRU = D_MIX // 2
LRU_BLOCKS = 8
LRU_BLOCK = D_LRU // LRU_BLOCKS
CONV_WIDTH = 4
LRU_C = 8.0
N_HEADS = 8
N_KV_HEADS = 2
GQA_GROUP = N_HEADS // N_KV_HEADS
HEAD_DIM = (D_MIX - D_LRU) // N_HEADS
D_ATTN = N_HEADS * HEAD_DIM
WINDOW = 128
BLOCK_Q = 128
D_FF = 2816
RMS_EPS = 1e-6
D_IN = 2 * D_LRU + (N_HEADS + 2 * N_KV_HEADS) * HEAD_DIM
MASK_VALUE = -1e30

kernel_name = "hybrid_rglru_swa_sink_macaron"


def rms_norm(x, g):
    xf = x.astype(jnp.float32)
    y = xf * lax.rsqrt(jnp.mean(xf * xf, axis=-1, keepdims=True) + RMS_EPS)
    return (y * g.astype(jnp.float32)).astype(x.dtype)


def swiglu(x, w_gu, w_down):
    gate, up = jnp.split(x @ w_gu, 2, axis=-1)
    return (jax.nn.silu(gate) * up) @ w_down


def causal_depthwise_conv(x, w, b):
    c = x.shape[-1]
    rhs = w.astype(x.dtype)[:, None, :]
    y = lax.conv_general_dilated(x, rhs, window_strides=(1,), padding=[(CONV_WIDTH - 1, 0)],
                                 dimension_numbers=("NWC", "WIO", "NWC"), feature_group_count=c)
    return y + b.astype(x.dtype)


def block_diag(x, w, b):
    bsz, s, _ = x.shape
    xh = x.reshape(bsz, s, LRU_BLOCKS, LRU_BLOCK)
    y = jnp.einsum("bshi,hij->bshj", xh, w).reshape(bsz, s, D_LRU)
    return y + b


def rg_lru(x_in, conv_w, conv_b, w_rg, b_rg, w_ig, b_ig, lam):
    xc = causal_depthwise_conv(x_in, conv_w, conv_b)
    r = jax.nn.sigmoid(block_diag(xc, w_rg, b_rg).astype(jnp.float32))
    i = jax.nn.sigmoid(block_diag(xc, w_ig, b_ig).astype(jnp.float32))
    log_a = LRU_C * r * jax.nn.log_sigmoid(lam.astype(jnp.float32))
    a = jnp.exp(log_a)
    mult = jnp.sqrt(-jnp.expm1(2.0 * log_a))
    u = mult * (i * xc.astype(jnp.float32))

    def combine(left, right):
        a_l, u_l = left
        a_r, u_r = right
        return a_l * a_r, a_r * u_l + u_r

    _, h = lax.associative_scan(combine, (a, u), axis=1)
    return h.astype(x_in.dtype)


def sliding_window_sink_attention(q, k, v, sinks):
    bsz, s = q.shape[0], q.shape[1]
    nb = s // BLOCK_Q
    qb = q.reshape(bsz, nb, BLOCK_Q, N_KV_HEADS, GQA_GROUP, HEAD_DIM)
    kb = k.reshape(bsz, nb, BLOCK_Q, N_KV_HEADS, HEAD_DIM)
    vb = v.reshape(bsz, nb, BLOCK_Q, N_KV_HEADS, HEAD_DIM)
    k_band = jnp.concatenate([jnp.concatenate([jnp.zeros_like(kb[:, :1]), kb[:, :-1]], axis=1), kb], axis=2)
    v_band = jnp.concatenate([jnp.concatenate([jnp.zeros_like(vb[:, :1]), vb[:, :-1]], axis=1), vb], axis=2)
    scale = HEAD_DIM ** -0.5
    scores = jnp.einsum("bnqkgd,bnjkd->bnkgqj", qb, k_band).astype(jnp.float32) * scale
    qi = jnp.arange(BLOCK_Q)[:, None]
    kj = jnp.arange(2 * BLOCK_Q)[None, :]
    rel = qi + BLOCK_Q - kj
    band = (rel >= 0) & (rel < WINDOW)
    blk = jnp.arange(nb)[:, None, None]
    mask = band[None] & ((blk > 0) | (kj >= BLOCK_Q)[None])
    scores = jnp.where(mask[None, :, None, None], scores, MASK_VALUE)
    sink = sinks.astype(jnp.float32).reshape(1, 1, N_KV_HEADS, GQA_GROUP, 1, 1)
    m = jnp.maximum(jnp.max(scores, axis=-1, keepdims=True), sink)
    p = jnp.exp(scores - m)
    denom = jnp.sum(p, axis=-1, keepdims=True) + jnp.exp(sink - m)
    p = (p / denom).astype(v.dtype)
    out = jnp.einsum("bnkgqj,bnjkd->bnqkgd", p, v_band)
    return out.reshape(bsz, s, D_ATTN)


def hybrid_mixer(h, w_in, conv_w, conv_b, w_rg, b_rg, w_ig, b_ig, lam, sinks, g_lru_out, g_attn_out, w_o):
    bsz, s, _ = h.shape
    proj = h @ w_in
    o1 = D_LRU
    o2 = o1 + D_LRU
    o3 = o2 + D_ATTN
    o4 = o3 + N_KV_HEADS * HEAD_DIM
    x_lru = proj[..., :o1]
    gate_lru = proj[..., o1:o2]
    q = proj[..., o2:o3].reshape(bsz, s, N_HEADS, HEAD_DIM)
    k = proj[..., o3:o4].reshape(bsz, s, N_KV_HEADS, HEAD_DIM)
    v = proj[..., o4:].reshape(bsz, s, N_KV_HEADS, HEAD_DIM)
    y_lru = rg_lru(x_lru, conv_w, conv_b, w_rg, b_rg, w_ig, b_ig, lam) * jax.nn.gelu(gate_lru)
    y_attn = sliding_window_sink_attention(q, k, v, sinks)
    y = jnp.concatenate([rms_norm(y_lru, g_lru_out), rms_norm(y_attn, g_attn_out)], axis=-1)
    return y @ w_o


def setup_inputs(seed: int = 0) -> dict:
    key = jax.random.key(seed)
    ks = jax.random.split(key, 32)
    f32 = jnp.float32

    def nrm(k, shape, scale):
        return jax.random.normal(k, shape, f32) * scale

    def gain(k, n):
        return 1.0 + 0.05 * jax.random.normal(k, (DEPTH, n), f32)

    u = jax.random.uniform(ks[13], (DEPTH, D_LRU), f32, 0.9, 0.999)
    return {
        "x": jax.random.normal(ks[0], (BATCH, SEQ, D_MODEL), f32),
        "ffn1_pre_g": gain(ks[1], D_MODEL),
        "ffn1_w_gu": nrm(ks[2], (DEPTH, D_MODEL, 2 * D_FF), D_MODEL ** -0.5),
        "ffn1_w_down": nrm(ks[3], (DEPTH, D_FF, D_MODEL), D_FF ** -0.5),
        "ffn1_post_g": gain(ks[4], D_MODEL),
        "mix_pre_g": gain(ks[5], D_MODEL),
        "w_in": nrm(ks[6], (DEPTH, D_MODEL, D_IN), D_MODEL ** -0.5),
        "conv_w": nrm(ks[7], (DEPTH, CONV_WIDTH, D_LRU), CONV_WIDTH ** -0.5),
        "conv_b": nrm(ks[8], (DEPTH, D_LRU), 0.01),
        "w_rg": nrm(ks[9], (DEPTH, LRU_BLOCKS, LRU_BLOCK, LRU_BLOCK), LRU_BLOCK ** -0.5),
        "b_rg": nrm(ks[10], (DEPTH, D_LRU), 0.01),
        "w_ig": nrm(ks[11], (DEPTH, LRU_BLOCKS, LRU_BLOCK, LRU_BLOCK), LRU_BLOCK ** -0.5),
        "b_ig": nrm(ks[12], (DEPTH, D_LRU), 0.01),
        "lru_lambda": jnp.log(u) - jnp.log1p(-u),
        "sinks": nrm(ks[14], (DEPTH, N_HEADS), 1.0),
        "g_lru_out": gain(ks[15], D_LRU),
        "g_attn_out": gain(ks[16], D_ATTN),
        "w_o": nrm(ks[17], (DEPTH, D_MIX, D_MODEL), D_MIX ** -0.5),
        "mix_post_g": gain(ks[18], D_MODEL),
        "ffn2_pre_g": gain(ks[19], D_MODEL),
        "ffn2_w_gu": nrm(ks[20], (DEPTH, D_MODEL, 2 * D_FF), D_MODEL ** -0.5),
        "ffn2_w_down": nrm(ks[21], (DEPTH, D_FF, D_MODEL), D_FF ** -0.5),
        "ffn2_post_g": gain(ks[22], D_MODEL),
    }


def reference(x, ffn1_pre_g, ffn1_w_gu, ffn1_w_down, ffn1_post_g, mix_pre_g, w_in, conv_w, conv_b,
              w_rg, b_rg, w_ig, b_ig, lru_lambda, sinks, g_lru_out, g_attn_out, w_o, mix_post_g,
              ffn2_pre_g, ffn2_w_gu, ffn2_w_down, ffn2_post_g):
    for l in range(DEPTH):
        x = x + 0.5 * rms_norm(swiglu(rms_norm(x, ffn1_pre_g[l]), ffn1_w_gu[l], ffn1_w_down[l]), ffn1_post_g[l])
        y = hybrid_mixer(rms_norm(x, mix_pre_g[l]), w_in[l], conv_w[l], conv_b[l], w_rg[l], b_rg[l],
                         w_ig[l], b_ig[l], lru_lambda[l], sinks[l], g_lru_out[l], g_attn_out[l], w_o[l])
        x = x + rms_norm(y, mix_post_g[l])
        x = x + 0.5 * rms_norm(swiglu(rms_norm(x, ffn2_pre_g[l]), ffn2_w_gu[l], ffn2_w_down[l]), ffn2_post_g[l])
    return x
```

```python
import contextlib
import sys
import numpy as np
import concourse.bass as bass
import concourse.mybir as mybir
from concourse.bass_utils import run_bass_kernel_spmd

F32 = mybir.dt.float32
BF16 = mybir.dt.bfloat16
AF = mybir.ActivationFunctionType
ALU = mybir.AluOpType

D = 1024
DFF = 2816
NJ = DFF // 128
SEQ = 8192
NT = 512
NTILES = 16
NOWN = 4
EPS = 1e-6
NSLOT = 5
PIPELINE = True
USE_PRECAST = False
PRECAST_PER_TILE = 5
XQ = "sp"
CONV_ENG = "dve"
SLOT_ELEMS = 2816

V_F1PRE, V_F1POST, V_MIXPRE, V_MIXPOST, V_F2PRE, V_F2POST = 0, 8, 16, 24, 32, 40
V_CONVW, V_CONVB, V_BRG, V_BIG, V_LAM, V_GLRU, V_GATTN, V_SINK, V_FLAGS = 48, 64, 68, 72, 76, 80, 84, 88, 92
NV = 92 + NTILES


class Op:
    __slots__ = ("eng", "fn", "stream", "deps", "sig", "sigval", "waits", "tag")


class Sched:
    def __init__(self):
        self.ops = []
        self.lastw = {}
        self.readers = {}
        self.group_streams = set()

    def add(self, eng, fn, reads=(), writes=(), stream=None):
        op = Op()
        op.eng, op.fn, op.stream = eng, fn, stream
        op.tag = sys._getframe(1).f_code.co_name
        op.sig, op.sigval, op.waits = False, 0, []
        deps = {}
        for k in reads:
            w = self.lastw.get(k)
            if w is not None:
                deps[id(w)] = (w, True)
        for k in writes:
            w = self.lastw.get(k)
            if w is not None and id(w) not in deps:
                deps[id(w)] = (w, False)
            for r in self.readers.get(k, ()):
                if id(r) not in deps and r is not op:
                    deps[id(r)] = (r, False)
        op.deps = []
        for (y, raw) in deps.values():
            if y.stream is None and stream is None and y.eng == eng and not raw:
                continue
            op.deps.append(y)
            y.sig = True
        for k in writes:
            self.lastw[k] = op
            self.readers[k] = []
        for k in reads:
            self.readers.setdefault(k, []).append(op)
        self.ops.append(op)
        return op

    def emit(self, nc, stack, final_streams):
        engs = ["pe", "act", "dve", "pool", "sp"]
        esem = {e: stack.enter_context(nc.semaphore("sem_" + e)) for e in engs}
        ssem = {}
        cnt = {e: 0 for e in engs}
        scnt = {}
        for op in self.ops:
            if op.stream is not None:
                if op.stream not in ssem:
                    ssem[op.stream] = stack.enter_context(nc.semaphore("st_" + op.stream))
                    scnt[op.stream] = 0
                scnt[op.stream] += 16
                op.sigval = scnt[op.stream]
            elif op.sig:
                cnt[op.eng] += 1
                op.sigval = cnt[op.eng]
        for op in self.ops:
            if op.stream in self.group_streams:
                op.sigval = scnt[op.stream]
        waited = {e: {} for e in engs}
        per = {e: [] for e in engs}
        for op in self.ops:
            w = waited[op.eng]
            need = {}
            for y in op.deps:
                s = ssem[y.stream] if y.stream is not None else esem[y.eng]
                key = id(s)
                if y.sigval > w.get(key, 0) and y.sigval > need.get(key, (None, 0))[1]:
                    need[key] = (s, y.sigval)
            for key, (s, v) in need.items():
                w[key] = v
                op.waits.append((s, v))
            per[op.eng].append(op)
        block = stack.enter_context(nc.Block())

        self.pe_tags = []

        class _Cnt:
            def __init__(self, e):
                self.e, self.n = e, 0

            def matmul(self, *a, **k):
                self.n += 1
                return self.e.matmul(*a, **k)

            def transpose(self, *a, **k):
                self.n += 1
                return self.e.transpose(*a, **k)

        def run(e, name):
            ce = _Cnt(e) if name == "pe" else e
            for op in per[name]:
                for (s, v) in op.waits:
                    e.wait_ge(s, v)
                if name == "pe":
                    n0 = ce.n
                ins = op.fn(ce)
                if name == "pe":
                    self.pe_tags.append((op.tag, n0, ce.n))
                if op.stream is not None:
                    ins.then_inc(ssem[op.stream], 16)
                elif op.sig:
                    ins.then_inc(esem[name], 1)
            if name == XQ:
                for st in final_streams:
                    if st in ssem:
                        e.wait_ge(ssem[st], scnt[st])

        block.tensor(lambda e: run(e, "pe"))
        block.scalar(lambda e: run(e, "act"))
        block.vector(lambda e: run(e, "dve"))
        block.gpsimd(lambda e: run(e, "pool"))
        block.sync(lambda e: run(e, "sp"))


def build_program(npre=12, nown=NOWN, stop_after="all"):
    ntiles = npre + nown
    nc = bass.Bass("TRN2", target_bir_lowering=False)
    dt_in = lambda n, s: nc.dram_tensor(n, s, F32, kind="ExternalInput").ap()
    xs = dt_in("xs", [ntiles * NT, D])
    wgu = [dt_in("wgu1", [NJ, 128, 2048]), dt_in("wgu2", [NJ, 128, 2048])]
    wd = [dt_in("wd1", [8, 128, 2816]), dt_in("wd2", [8, 128, 2816])]
    win = dt_in("win", [14, 128, 1024])
    wo = dt_in("wo", [8, 128, 1024])
    vecs_d = dt_in("vecs", [128, NV])
    wg_d = dt_in("wg", [128, 8 * 128])
    masks_d = dt_in("masks", [128, 3 * 128])
    ident_d = dt_in("ident", [128, 128])
    out = nc.dram_tensor("out", [nown * NT, D], F32, kind="ExternalOutput").ap()
    dt_bf = lambda n, s: nc.dram_tensor(n, s, BF16).ap()
    wgu_b = [dt_bf("wgu1_bf", [NJ, 128, 2048]), dt_bf("wgu2_bf", [NJ, 128, 2048])]
    wd_b = [dt_bf("wd1_bf", [8, 128, 2816]), dt_bf("wd2_bf", [8, 128, 2816])]
    win_b = dt_bf("win_bf", [14, 128, 1024])
    wo_b = dt_bf("wo_bf", [8, 128, 1024])

    S = Sched()
    stack = contextlib.ExitStack()
    with stack:
        sb = lambda n, s, d: stack.enter_context(nc.sbuf_tensor("s_" + n, s, d))
        xT = [sb("xT%d" % i, [128, 8, NT], F32) for i in range(2)]
        xst = [sb("xst%d" % i, [128, D], F32) for i in range(2)]
        hnf = [sb("hnf%d" % i, [128, 8, NT], BF16) for i in range(2)]
        hnm = sb("hnm", [128, 8, NT], BF16)
        sq = [sb("sq%d" % i, [128, NT], BF16) for i in range(12)]
        rs = [sb("rs%d" % i, [128, NT], F32) for i in range(2)]
        act = sb("act", [128, NJ, NT], BF16)
        sg = [sb("sg%d" % i, [128, NT], F32) for i in range(2)]
        y = sb("y", [128, 8, NT], F32)
        ring = [sb("ring%d" % i, [128, SLOT_ELEMS], BF16) for i in range(NSLOT)]
        xl = sb("xl", [128, 4, NT + 3], F32)
        xc = sb("xc", [128, NT], F32)
        xcb = sb("xcb", [128, NT], BF16)
        r_t = sb("r_t", [128, NT], F32)
        i_t = sb("i_t", [128, NT], F32)
        a_t = sb("a_t", [128, NT], F32)
        m_t = sb("m_t", [128, NT], F32)
        hseq = sb("hseq", [128, NT], F32)
        hstate = sb("hstate", [128, 4], F32)
        hsin = sb("hsin", [128, 4], F32)
        gl = sb("gl", [128, NT], F32)
        ycat = sb("ycat", [128, 8, NT], F32)
        qT = sb("qT", [128, 4, NT], BF16)
        kT = sb("kT", [128, 128 + NT], BF16)
        vtok = sb("vtok", [128, 5, 128], BF16)
        pbuf = [sb("p%d" % i, [128, NT], BF16) for i in range(4)]
        dn = sb("dn", [128, NT], F32)
        vec = sb("vec", [128, NV], F32)
        gh = sb("gh", [128, 24], F32)
        c8 = sb("c8", [128, 4], F32)
        epsb = sb("epsb", [128, 1], F32)
        oneb = sb("oneb", [128, 1], F32)
        tmp4 = [sb("tmp4_%d" % i, [128, 4], F32) for i in range(4)]
        esink = sb("esink", [128, 4], F32)
        wgb = sb("wgb", [128, 8, 128], BF16)
        masks = sb("masks", [128, 3, 128], BF16)
        ident = sb("ident", [128, 128], F32)
        ones = sb("ones", [128, 128], BF16)
        ps = [stack.enter_context(nc.psum_tensor("ps%d" % i, [128, NT], F32)) for i in range(8)]

        rr = {"bank": 0, "slot": 0, "sq": 0, "rs": 0, "sg": 0, "p": 0, "sqm": 0, "sqs": 0}
        sq_alloc = {"n": lambda: nxt("sq", 8), "m": lambda: 8 + nxt("sqm", 2), "s": lambda: 10 + nxt("sqs", 2)}

        def nxt(name, n):
            v = rr[name]
            rr[name] = (v + 1) % n
            return v

        rr["mbank"] = 0
        bank_mode = {"split": False}
        rr["sbank"] = 0

        def nb():
            return 4 + nxt("sbank", 3) if bank_mode["split"] else nxt("bank", 7)

        def nbm():
            return nxt("mbank", 4) if bank_mode["split"] else nxt("bank", 7)

        SSB1 = 7

        S.add("sp", lambda e: e.dma_start(out=vec[:], in_=vecs_d[:, :]), writes=["vec"], stream="c_vec")
        S.add("sp", lambda e: e.dma_start(out=ident[:], in_=ident_d[:, :]), writes=["ident"], stream="c_id")
        S.add("pool", lambda e: e.dma_start(out=wgb[:].rearrange("p a b -> p (a b)"), in_=wg_d[:, :]),
              writes=["wgb"], stream="c_wg")
        S.add("pool", lambda e: e.dma_start(out=masks[:].rearrange("p a b -> p (a b)"), in_=masks_d[:, :]),
              writes=["masks"], stream="c_mask")
        S.add("dve", lambda e: e.memset(ones[:], 1.0), writes=["ones"])
        S.add("dve", lambda e: e.memset(epsb[:], EPS), writes=["epsb"])
        S.add("dve", lambda e: e.memset(oneb[:], 1.0), writes=["oneb"])
        S.add("dve", lambda e: e.memset(xl[:].rearrange("p a b -> p (a b)"), 0.0), writes=[("xl", c) for c in range(4)])
        S.add("dve", lambda e: e.memset(hstate[:], 0.0), writes=["hstate"])
        S.add("dve", lambda e: e.memset(kT[:], 0.0), writes=["kT"])
        S.add("dve", lambda e: e.memset(vtok[:].rearrange("p a b -> p (a b)"), 0.0), writes=["vtok"])
        S.add("dve", lambda e: e.tensor_scalar(out=gh[:, 0:8], in0=vec[:, V_F1POST:V_F1POST + 8], scalar1=0.5,
                                               scalar2=None, op0=ALU.mult), reads=["vec"], writes=["gh0"])
        S.add("dve", lambda e: e.tensor_copy(out=gh[:, 8:16], in_=vec[:, V_MIXPOST:V_MIXPOST + 8]),
              reads=["vec"], writes=["gh1"])
        S.add("dve", lambda e: e.tensor_scalar(out=gh[:, 16:24], in0=vec[:, V_F2POST:V_F2POST + 8], scalar1=0.5,
                                               scalar2=None, op0=ALU.mult), reads=["vec"], writes=["gh2"])
        t0_, t1_, t2_, t3_ = tmp4
        lam = vec[:, V_LAM:V_LAM + 4]
        S.add("act", lambda e: e.activation(out=t0_[:], in_=lam, func=AF.Exp, scale=-1.0), reads=["vec"], writes=["t0"])
        S.add("dve", lambda e: e.tensor_scalar(out=t1_[:], in0=t0_[:], scalar1=1.0, scalar2=None, op0=ALU.add),
              reads=["t0"], writes=["t1"])
        S.add("act", lambda e: e.activation(out=t2_[:], in_=t1_[:], func=AF.Ln), reads=["t1"], writes=["t2"])
        S.add("dve", lambda e: e.tensor_scalar(out=t3_[:], in0=t1_[:], scalar1=-1.0, scalar2=None, op0=ALU.add),
              reads=["t1"], writes=["t3"])
        S.add("dve", lambda e: e.reciprocal(out=t3_[:], in_=t3_[:]), reads=["t3"], writes=["t3"])
        S.add("dve", lambda e: e.tensor_tensor(out=t3_[:], in0=t3_[:], in1=t0_[:], op=ALU.mult),
              reads=["t3", "t0"], writes=["t3"])
        S.add("dve", lambda e: e.tensor_tensor(out=t3_[:], in0=t3_[:], in1=t2_[:], op=ALU.mult),
              reads=["t3", "t2"], writes=["t3"])
        S.add("dve", lambda e: e.tensor_scalar(out=c8[:], in0=t3_[:], scalar1=-8.0, scalar2=None, op0=ALU.mult),
              reads=["t3"], writes=["c8"])
        S.add("act", lambda e: e.activation(out=esink[:], in_=vec[:, V_SINK:V_SINK + 4], func=AF.Exp),
              reads=["vec"], writes=["esink"])

        precast_done = set()
        bf_ok = set()

        def precast(name, j, group):
            src32, srcbf = WSRC[name]
            S.group_streams.add(group)
            S.add("pool", lambda e: e.dma_start(out=srcbf[j], in_=src32[j], max_dma_last_dim=4096),
                  writes=[("wb", name, j)], stream=group)
            precast_done.add((name, j))

        def wload(name, j, nel):
            src32, srcbf = WSRC[name]
            slot = nxt("slot", NSLOT)
            if (name, j) in bf_ok:
                S.add("pool", lambda e: e.dma_start(out=ring[slot][:, 0:nel], in_=srcbf[j]),
                      reads=[("wb", name, j)], writes=[("ring", slot)], stream="ringS%d" % slot)
            else:
                S.add("pool", lambda e: e.dma_start(out=ring[slot][:, 0:nel], in_=src32[j], max_dma_last_dim=4096),
                      writes=[("ring", slot)], stream="ringS%d" % slot)
            return slot

        WSRC = {"gu0": (wgu[0], wgu_b[0]), "gu1": (wgu[1], wgu_b[1]), "d0": (wd[0], wd_b[0]), "d1": (wd[1], wd_b[1]),
                "win": (win, win_b), "wo": (wo, wo_b)}

        def sumsq_squares(src_fn, src_keys, nchunks):
            sis = []
            for k in range(nchunks):
                si = sq_alloc["n"]()
                sis.append(si)
                S.add("act", lambda e, k=k, si=si: e.activation(out=sq[si][:], in_=src_fn(k), func=AF.Square),
                      reads=[src_keys[k]], writes=[("sq", si)])
            return sis

        def sumsq_finish(sis, inv_n):
            bk = nb()
            n = len(sis)
            for k, si in enumerate(sis):
                S.add("pe", lambda e, k=k, si=si: e.matmul(ps[bk][:], lhsT=ones[:], rhs=sq[si][:], start=(k == 0),
                                                          stop=(k == n - 1)),
                      reads=[("sq", si), "ones"], writes=[("ps", bk)])
            return finish_rstd(bk, inv_n)

        def finish_rstd(bk, inv_n):
            ri = nxt("rs", 2)
            S.add("act", lambda e: e.activation(out=rs[ri][:], in_=ps[bk][:], func=AF.Ln, scale=inv_n, bias=epsb[:, 0:1]),
                  reads=[("ps", bk), "epsb"], writes=[("rs", ri)])
            S.add("act", lambda e: e.activation(out=rs[ri][:], in_=rs[ri][:], func=AF.Exp, scale=-0.5),
                  reads=[("rs", ri)], writes=[("rs", ri)])
            return ri

        def gen_prenorm(par, gcol, dst, dkey, split=True):
            xt = xT[par]
            if split:
                sis = sumsq_squares(lambda k: xt[:, k, :], [("xT", par, k) for k in range(8)], 8)
                yield
                yield
                ri = sumsq_finish(sis, 1.0 / D)
            else:
                bk = nbm()
                for k in range(8):
                    si = sq_alloc["m"]()
                    S.add("act", lambda e, k=k, si=si: e.activation(out=sq[si][:], in_=xt[:, k, :], func=AF.Square),
                          reads=[("xT", par, k)], writes=[("sq", si)])
                    S.add("pe", lambda e, k=k, si=si: e.matmul(ps[bk][:], lhsT=ones[:], rhs=sq[si][:], start=(k == 0),
                                                              stop=(k == 7)),
                          reads=[("sq", si), "ones"], writes=[("ps", bk)])
                ri = finish_rstd(bk, 1.0 / D)
            for k in range(8):
                S.add("dve", lambda e, k=k: e.scalar_tensor_tensor(out=dst[:, k, :], in0=xt[:, k, :],
                                                                  scalar=vec[:, gcol + k:gcol + k + 1], in1=rs[ri][:],
                                                                  op0=ALU.mult, op1=ALU.mult),
                      reads=[("xT", par, k), ("rs", ri), "vec"], writes=[dkey(k)])
            yield

        def gen_outproj(group_fn, ybuf, ykey, ssb, pool):
            pend = None
            for m in range(8):
                bk = group_fn(m)
                si = sq_alloc[pool]()
                S.add("act", lambda e, m=m, bk=bk: e.activation(out=ybuf[:, m, :], in_=ps[bk][:], func=AF.Copy),
                      reads=[("ps", bk)], writes=[(ykey, m)])
                S.add("act", lambda e, bk=bk, si=si: e.activation(out=sq[si][:], in_=ps[bk][:], func=AF.Square),
                      reads=[("ps", bk)], writes=[("sq", si)])
                if pend is not None:
                    pm, psi = pend
                    S.add("pe", lambda e, pm=pm, psi=psi: e.matmul(ps[ssb][:], lhsT=ones[:], rhs=sq[psi][:],
                                                                  start=(pm == 0), stop=False),
                          reads=[("sq", psi), "ones"], writes=[("ps", ssb)])
                pend = (m, si)
                yield
            pm, psi = pend
            S.add("pe", lambda e: e.matmul(ps[ssb][:], lhsT=ones[:], rhs=sq[psi][:], start=False, stop=True),
                  reads=[("sq", psi), "ones"], writes=[("ps", ssb)])
            yield

        def gen_residual(par, ybuf, ykey, ssb, ghcol, ghkey, ri=None):
            xt = xT[par]
            if ri is None:
                ri = finish_rstd(ssb, 1.0 / D)
            for k in range(8):
                S.add("dve", lambda e, k=k: e.scalar_tensor_tensor(out=ybuf[:, k, :], in0=ybuf[:, k, :],
                                                                  scalar=gh[:, ghcol + k:ghcol + k + 1], in1=rs[ri][:],
                                                                  op0=ALU.mult, op1=ALU.mult),
                      reads=[(ykey, k), ("rs", ri), ghkey], writes=[(ykey, k)])
            yield
            for k in range(8):
                S.add("dve", lambda e, k=k: e.tensor_tensor(out=xt[:, k, :], in0=xt[:, k, :], in1=ybuf[:, k, :], op=ALU.add),
                      reads=[(ykey, k), ("xT", par, k)], writes=[("xT", par, k)])
            yield

        def gen_ffn_main(which, par):
            hn = hnf[par]
            hk = [("hnf", par, k) for k in range(8)]
            for j in range(NJ):
                slot = wload("gu%d" % which, j, 2048)
                bg = nbm()
                bu = nbm()

                def mm(e, slot=slot, bk=bg, off=0):
                    for k in range(8):
                        ins = e.matmul(ps[bk][:], lhsT=ring[slot][:, k * 256 + off:k * 256 + off + 128], rhs=hn[:, k, :],
                                       start=(k == 0), stop=(k == 7))
                    return ins
                S.add("pe", mm, reads=[("ring", slot)] + hk, writes=[("ps", bg)])
                S.add("pe", lambda e, mm=mm, slot=slot, bu=bu: mm(e, slot, bu, 128), reads=[("ring", slot)] + hk,
                      writes=[("ps", bu)])
                gi = nxt("sg", 2)
                S.add("act", lambda e, bg=bg, gi=gi: e.activation(out=sg[gi][:], in_=ps[bg][:], func=AF.Silu),
                      reads=[("ps", bg)], writes=[("sg", gi)])
                S.add("dve", lambda e, bu=bu, gi=gi, j=j: e.tensor_tensor(out=act[:, j, :], in0=ps[bu][:], in1=sg[gi][:],
                                                                         op=ALU.mult),
                      reads=[("ps", bu), ("sg", gi)], writes=[("act", j)])
                yield

            def down(m):
                slot = wload("d%d" % which, m, 2816)
                bk = nbm()

                def mm(e, lo=0, hi=NJ):
                    for kk in range(lo, hi):
                        ins = e.matmul(ps[bk][:], lhsT=ring[slot][:, kk * 128:(kk + 1) * 128], rhs=act[:, kk, :],
                                       start=(kk == 0), stop=(kk == NJ - 1))
                    return ins
                sp = NJ - 4
                S.add("pe", lambda e: mm(e, 0, sp), reads=[("ring", slot)] + [("act", kk) for kk in range(sp)],
                      writes=[("ps", bk)])
                S.add("pe", lambda e: mm(e, sp, NJ), reads=[("ring", slot)] + [("act", kk) for kk in range(sp, NJ)],
                      writes=[("ps", bk)])
                return bk
            yield from gen_outproj(down, y, "y", SSB1, "m")

        def proj_chunk(n):
            slot = wload("win", n, 1024)
            bk = nb()

            def mm(e):
                for k in range(8):
                    ins = e.matmul(ps[bk][:], lhsT=ring[slot][:, k * 128:(k + 1) * 128], rhs=hnm[:, k, :],
                                   start=(k == 0), stop=(k == 7))
                return ins
            S.add("pe", mm, reads=[("ring", slot)] + [("hnm", k) for k in range(8)], writes=[("ps", bk)])
            return bk

        def gen_lru(t, own):
            fl = vec[:, V_FLAGS + t:V_FLAGS + t + 1]
            for c in range(4):
                bk = proj_chunk(c)
                S.add("dve", lambda e, c=c: e.tensor_scalar(out=xl[:, c, 0:3], in0=xl[:, c, NT:NT + 3], scalar1=fl,
                                                            scalar2=None, op0=ALU.mult),
                      reads=[("xl", c), "vec"], writes=[("xl", c)])
                S.add("act", lambda e, c=c, bk=bk: e.activation(out=xl[:, c, 3:NT + 3], in_=ps[bk][:], func=AF.Copy),
                      reads=[("ps", bk)], writes=[("xl", c)])
                if c % 2 == 1:
                    yield
            for c in range(4):
                yield from gen_lru_chunk(t, c, own, fl)

        def gen_lru_chunk(t, c, own, fl):
            cw = lambda k: vec[:, V_CONVW + c * 4 + k:V_CONVW + c * 4 + k + 1]
            S.add(CONV_ENG, lambda e: e.tensor_scalar(out=xc[:], in0=xl[:, c, 3:NT + 3], scalar1=cw(3),
                                                   scalar2=vec[:, V_CONVB + c:V_CONVB + c + 1], op0=ALU.mult, op1=ALU.add),
                  reads=[("xl", c), "vec"], writes=["xc"])
            for k in range(3):
                if CONV_ENG == "dve":
                    S.add("dve", lambda e, k=k: e.scalar_tensor_tensor(out=xc[:], in0=xl[:, c, k:k + NT], scalar=cw(k),
                                                                      in1=xc[:], op0=ALU.mult, op1=ALU.add),
                          reads=[("xl", c), "vec", "xc"], writes=["xc"])
                else:
                    S.add("pool", lambda e, k=k: e.tensor_scalar(out=a_t[:], in0=xl[:, c, k:k + NT], scalar1=cw(k),
                                                                 scalar2=0.0, op0=ALU.mult, op1=ALU.add),
                          reads=[("xl", c), "vec"], writes=["a_t"])
                    S.add("pool", lambda e: e.tensor_tensor(out=xc[:], in0=xc[:], in1=a_t[:], op=ALU.add),
                          reads=["xc", "a_t"], writes=["xc"])
            S.add("act", lambda e: e.activation(out=xcb[:], in_=xc[:], func=AF.Copy), reads=["xc"], writes=["xcb"])
            if own:
                bgk = proj_chunk(4 + c)
                S.add("act", lambda e: e.activation(out=gl[:], in_=ps[bgk][:], func=AF.Gelu),
                      reads=[("ps", bgk)], writes=["gl"])
            yield
            yield
            yield
            br = nb()
            bi = nb()
            S.add("pe", lambda e: e.matmul(ps[br][:], lhsT=wgb[:, c, :], rhs=xcb[:], start=True, stop=True),
                  reads=["xcb", "wgb"], writes=[("ps", br)])
            S.add("pe", lambda e: e.matmul(ps[bi][:], lhsT=wgb[:, 4 + c, :], rhs=xcb[:], start=True, stop=True),
                  reads=["xcb", "wgb"], writes=[("ps", bi)])
            S.add("act", lambda e: e.activation(out=r_t[:], in_=ps[br][:], func=AF.Sigmoid,
                                                bias=vec[:, V_BRG + c:V_BRG + c + 1]),
                  reads=[("ps", br), "vec"], writes=["r_t"])
            S.add("act", lambda e: e.activation(out=i_t[:], in_=ps[bi][:], func=AF.Sigmoid,
                                                bias=vec[:, V_BIG + c:V_BIG + c + 1]),
                  reads=[("ps", bi), "vec"], writes=["i_t"])
            S.add("act", lambda e: e.activation(out=a_t[:], in_=r_t[:], func=AF.Exp, scale=c8[:, c:c + 1]),
                  reads=["r_t", "c8"], writes=["a_t"])
            S.add("act", lambda e: e.activation(out=m_t[:], in_=a_t[:], func=AF.Square), reads=["a_t"], writes=["m_t"])
            S.add("act", lambda e: e.activation(out=m_t[:], in_=m_t[:], func=AF.Ln, scale=-1.0, bias=oneb[:, 0:1]),
                  reads=["m_t", "oneb"], writes=["m_t"])
            S.add("act", lambda e: e.activation(out=m_t[:], in_=m_t[:], func=AF.Exp, scale=0.5),
                  reads=["m_t"], writes=["m_t"])
            S.add("dve", lambda e: e.tensor_tensor(out=i_t[:], in0=i_t[:], in1=xc[:], op=ALU.mult),
                  reads=["i_t", "xc"], writes=["i_t"])
            S.add("dve", lambda e: e.tensor_tensor(out=i_t[:], in0=i_t[:], in1=m_t[:], op=ALU.mult),
                  reads=["i_t", "m_t"], writes=["i_t"])
            S.add("dve", lambda e: e.tensor_scalar(out=hsin[:, c:c + 1], in0=hstate[:, c:c + 1], scalar1=fl, scalar2=None,
                                                   op0=ALU.mult), reads=["hstate", "vec"], writes=["hsin"])
            S.add("dve", lambda e: e.tensor_tensor_scan(out=hseq[:], data0=a_t[:], data1=i_t[:], initial=hsin[:, c:c + 1],
                                                        op0=ALU.mult, op1=ALU.add),
                  reads=["a_t", "i_t", "hsin"], writes=["hseq"])
            S.add("dve", lambda e: e.tensor_copy(out=hstate[:, c:c + 1], in_=hseq[:, NT - 1:NT]),
                  reads=["hseq"], writes=["hstate"])
            if own:
                S.add("dve", lambda e: e.tensor_tensor(out=ycat[:, c, :], in0=hseq[:], in1=gl[:], op=ALU.mult),
                      reads=["hseq", "gl"], writes=[("ycat", c)])
            yield

        def gen_kv_proj():
            bk = proj_chunk(12)
            S.add("dve", lambda e: e.tensor_copy(out=kT[:, 0:128], in_=kT[:, NT:NT + 128]), reads=["kT"], writes=["kT"])
            S.add("act", lambda e: e.activation(out=kT[:, 128:128 + NT], in_=ps[bk][:], func=AF.Copy),
                  reads=[("ps", bk)], writes=["kT"])
            yield
            slot = wload("win", 13, 1024)
            bv = nb()

            def mm(e):
                for b in range(4):
                    for k in range(8):
                        ins = e.matmul(ps[bv][:, b * 128:(b + 1) * 128], lhsT=hnm[:, k, b * 128:(b + 1) * 128],
                                       rhs=ring[slot][:, k * 128:(k + 1) * 128], start=(k == 0), stop=(k == 7))
                return ins
            S.add("pe", mm, reads=[("ring", slot)] + [("hnm", k) for k in range(8)], writes=[("ps", bv)])
            S.add("dve", lambda e: e.tensor_copy(out=vtok[:, 0, :], in_=vtok[:, 4, :]), reads=["vtok"], writes=["vtok"])
            S.add("act", lambda e: e.activation(out=vtok[:, 1:5, :], in_=ps[bv][:].rearrange("p (a b) -> p a b", a=4),
                                                func=AF.Copy), reads=[("ps", bv)], writes=["vtok"])
            yield

        def gen_attention(first_own):
            for c in range(4):
                bk = proj_chunk(8 + c)
                S.add("act", lambda e, c=c, bk=bk: e.activation(out=qT[:, c, :], in_=ps[bk][:], func=AF.Copy, scale=0.125),
                      reads=[("ps", bk)], writes=["qT"])
                if c % 2 == 1:
                    yield
            yield from gen_kv_proj()
            yield
            for qb in range(4):
                yield from gen_attn_block(qb, first_own and qb == 0)

        def gen_attn_block(qb, use_first_mask):
            v3 = lambda ap: ap.rearrange("p (a b) -> p a b", a=4)
            pis = {}
            for kv in range(2):
                pr = slice(kv * 64, (kv + 1) * 64)
                for kb in range(2):
                    bk = nb()
                    kc = (qb + kb) * 128
                    S.add("pe", lambda e, bk=bk, pr=pr, kc=kc: e.matmul(
                        ps[bk][:], lhsT=kT[pr, kc:kc + 128], rhs=qT[pr, :, qb * 128:(qb + 1) * 128],
                        start=True, stop=True), reads=["kT", "qT"], writes=[("ps", bk)])
                    pi = nxt("p", 4)
                    pis[(kv, kb)] = pi
                    S.add("act", lambda e, bk=bk, pi=pi: e.activation(out=pbuf[pi][:], in_=ps[bk][:], func=AF.Exp),
                          reads=[("ps", bk)], writes=[("p", pi)])
                    mi = 0 if kb == 1 else (2 if use_first_mask else 1)
                    S.add("dve", lambda e, pi=pi, mi=mi: e.tensor_tensor(
                        out=v3(pbuf[pi][:]), in0=v3(pbuf[pi][:]),
                        in1=masks[:, mi, :].unsqueeze(1).to_broadcast([128, 4, 128]), op=ALU.mult),
                        reads=[("p", pi), "masks"], writes=[("p", pi)])
                yield
            yield
            ob = nb()
            db = nb()
            for kv in range(2):
                pr = slice(kv * 64, (kv + 1) * 64)

                def mm(e, kv=kv, pr=pr):
                    for kb in range(2):
                        e.matmul(ps[ob][pr, :], lhsT=vtok[:, qb + kb, kv * 64:(kv + 1) * 64],
                                 rhs=pbuf[pis[(kv, kb)]][:], start=(kb == 0), stop=(kb == 1))
                    for kb in range(2):
                        ins = e.matmul(ps[db][pr, :], lhsT=ones[:, 0:64], rhs=pbuf[pis[(kv, kb)]][:],
                                       start=(kb == 0), stop=(kb == 1))
                    return ins
                S.add("pe", mm, reads=["vtok", "ones", ("p", pis[(kv, 0)]), ("p", pis[(kv, 1)])],
                      writes=[("ps", ob), ("ps", db)])
            ysl = lambda: ycat[:, 4:8, qb * 128:(qb + 1) * 128]
            S.add("act", lambda e: e.activation(out=ysl(), in_=v3(ps[ob][:]), func=AF.Copy),
                  reads=[("ps", ob)], writes=[("ycat", 4 + c) for c in range(4)])
            S.add("dve", lambda e: e.tensor_tensor(out=v3(dn[:]), in0=v3(ps[db][:]),
                                                   in1=esink[:, 0:4].unsqueeze(2).to_broadcast([128, 4, 128]),
                                                   op=ALU.add), reads=[("ps", db), "esink"], writes=["dn"])
            S.add("dve", lambda e: e.reciprocal(out=dn[:], in_=dn[:]), reads=["dn"], writes=["dn"])
            S.add("dve", lambda e: e.tensor_tensor(out=ysl(), in0=ysl(), in1=v3(dn[:]), op=ALU.mult),
                  reads=[("ycat", 4 + c) for c in range(4)] + ["dn"], writes=[("ycat", 4 + c) for c in range(4)])
            yield

        def gen_mixer_tail(par):
            for half, gcol in ((0, V_GLRU), (1, V_GATTN)):
                sis = sumsq_squares(lambda k, half=half: ycat[:, half * 4 + k, :],
                                    [("ycat", half * 4 + k) for k in range(4)], 4)
                yield
                yield
                ri = sumsq_finish(sis, 1.0 / 512)
                for k in range(4):
                    kk = half * 4 + k
                    S.add("dve", lambda e, kk=kk, k=k, gcol=gcol, ri=ri: e.scalar_tensor_tensor(
                        out=hnm[:, kk, :], in0=ycat[:, kk, :], scalar=vec[:, gcol + k:gcol + k + 1], in1=rs[ri][:],
                        op0=ALU.mult, op1=ALU.mult), reads=[("ycat", kk), ("rs", ri), "vec"], writes=[("hnm", kk)])
                yield

            def oproj(m):
                slot = wload("wo", m, 1024)
                bk = nb()

                def mm(e):
                    for kk in range(8):
                        ins = e.matmul(ps[bk][:], lhsT=ring[slot][:, kk * 128:(kk + 1) * 128], rhs=hnm[:, kk, :],
                                       start=(kk == 0), stop=(kk == 7))
                    return ins
                S.add("pe", mm, reads=[("ring", slot)] + [("hnm", k) for k in range(8)], writes=[("ps", bk)])
                return bk
            yield
            for m in range(8):
                bk = oproj(m)
                S.add("act", lambda e, m=m, bk=bk: e.activation(out=ycat[:, m, :], in_=ps[bk][:], func=AF.Copy),
                      reads=[("ps", bk)], writes=[("ycat", m)])
                if m % 2 == 1:
                    yield
            sis = sumsq_squares(lambda k: ycat[:, k, :], [("ycat", k) for k in range(8)], 8)
            yield
            yield
            ri = sumsq_finish(sis, 1.0 / D)
            yield from gen_residual(par, ycat, "ycat", None, 8, "gh1", ri=ri)

        def gen_load_x(t, par):
            xt = xT[par]

            def dma(b):
                xi = b % 2
                r0 = t * NT + b * 128
                S.add("sp", lambda e: e.dma_start(out=xst[xi][:], in_=xs[r0:r0 + 128, :]),
                      writes=[("xst", xi)], stream="xld%d" % xi)

            def tr_block(b):
                xi = b % 2
                for half in range(2):
                    bk = nb()

                    def tr(e, bk=bk, half=half):
                        for kk in range(4):
                            k = half * 4 + kk
                            ins = e.transpose(out=ps[bk][:, kk * 128:(kk + 1) * 128], in_=xst[xi][:, k * 128:(k + 1) * 128],
                                              identity=ident[:])
                        return ins
                    S.add("pe", tr, reads=[("xst", xi), "ident"], writes=[("ps", bk)])
                    S.add("act", lambda e, bk=bk, half=half: e.activation(
                        out=xt[:, half * 4:(half + 1) * 4, b * 128:(b + 1) * 128],
                        in_=ps[bk][:].rearrange("p (a b) -> p a b", a=4), func=AF.Copy),
                        reads=[("ps", bk)], writes=[("xT", par, half * 4 + kk) for kk in range(4)])
            dma(0)
            dma(1)
            yield
            for b in range(4):
                tr_block(b)
                if b + 2 < 4:
                    dma(b + 2)
                yield

        def gen_store_x(ot, par):
            xt = xT[par]
            for b in range(4):
                xi = b % 2
                bks = [nb(), nb()]

                def tr(e, bks=bks, b=b):
                    for k in range(8):
                        ins = e.transpose(out=ps[bks[k // 4]][:, (k % 4) * 128:(k % 4 + 1) * 128],
                                          in_=xt[:, k, b * 128:(b + 1) * 128], identity=ident[:])
                    return ins
                S.add("pe", tr, reads=[("xT", par, k) for k in range(8)] + ["ident"],
                      writes=[("ps", bks[0]), ("ps", bks[1])])
                for half in range(2):
                    S.add("act", lambda e, xi=xi, half=half, bks=bks: e.activation(
                        out=xst[xi][:, half * 512:(half + 1) * 512], in_=ps[bks[half]][:], func=AF.Copy),
                        reads=[("ps", bks[half])], writes=[("xst", xi)])
                r0 = ot * NT + b * 128
                S.add(XQ, lambda e, xi=xi, r0=r0: e.dma_start(out=out[r0:r0 + 128, :], in_=xst[xi][:]),
                      reads=[("xst", xi)], stream="xsto%d" % xi)
                yield

        def gen_A0(t):
            par = t % 2
            yield from gen_load_x(t, par)
            yield from gen_prenorm(par, V_F1PRE, hnf[par], lambda k: ("hnf", par, k))

        def roundrobin(g1, g2):
            live = [g1, g2]
            while live:
                for g in list(live):
                    try:
                        next(g)
                        yield
                    except StopIteration:
                        live.remove(g)

        def gen_Bhead(t):
            par = t % 2
            yield from gen_residual(par, y, "y", SSB1, 0, "gh0")
            if t >= npre and stop_after == "ffn1":
                return
            yield from gen_prenorm(par, V_MIXPRE, hnm, lambda k: ("hnm", k))

        def gen_Btail(t):
            par = t % 2
            own = t >= npre
            if own and stop_after == "ffn1":
                return
            if own:
                yield from roundrobin(gen_lru(t, own), gen_attention(first_own=(t == npre)))
                yield from gen_mixer_tail(par)
            else:
                yield from gen_lru(t, own)
            if (not own) and t == npre - 1:
                yield from gen_kv_proj()

        def gen_B(t):
            yield from gen_Bhead(t)
            yield
            yield from gen_Btail(t)
            if t >= npre and stop_after == "all":
                par = t % 2
                yield from gen_prenorm(par, V_F2PRE, hnf[par], lambda k: ("hnf", par, k))

        def gen_C(t):
            par = t % 2
            if not (PIPELINE and stop_after == "all" and nown == 4):
                yield from gen_prenorm(par, V_F2PRE, hnf[par], lambda k: ("hnf", par, k), split=False)
            yield from gen_ffn_main(1, par)

        def gen_Cfin(t):
            yield from gen_residual(t % 2, y, "y", SSB1, 16, "gh2")

        def run(g):
            for _ in g:
                pass

        def delay(n):
            for _ in range(n):
                yield

        def chain(*gs):
            for g in gs:
                yield from g

        def interleave(main, side, ratio=1):
            for _ in main:
                for _ in range(ratio):
                    next(side, None)
            run(side)

        full = stop_after == "all"
        if not (PIPELINE and full and nown == 4):
            for t in range(ntiles):
                run(gen_A0(t))
                run(gen_ffn_main(0, t % 2))
                run(gen_B(t))
                if t >= npre:
                    if full:
                        run(gen_C(t))
                        run(gen_Cfin(t))
                    run(gen_store_x(t - npre, t % 2))
        else:
            A1 = lambda t: gen_ffn_main(0, t % 2)
            ST = lambda t: gen_store_x(t - npre, t % 2)
            run(gen_A0(0))
            pc_plan = ([("gu1", j) for j in range(NJ)] + [("d1", m) for m in range(8)] + [("wo", m) for m in range(8)]
                       + [("win", n) for n in range(4, 14)])
            if npre < 2 or not USE_PRECAST:
                pc_plan = []
            per_tile = -(-len(pc_plan) // max(1, npre - 1))
            for t in range(npre):
                if USE_PRECAST and t >= 1:
                    for _ in range(per_tile):
                        if pc_plan:
                            precast(*pc_plan.pop(0), "pc_own")
                side = []
                if t >= 1:
                    side.append(gen_Bhead(t - 1))
                side.append(gen_A0(t + 1))
                if t >= 1:
                    side.append(gen_Btail(t - 1))
                interleave(A1(t), chain(*side), 1)
            o0, o1, o2, o3 = range(npre, npre + 4)
            if not pc_plan:
                bf_ok.update(precast_done)
            bank_mode["split"] = False
            first_side = [gen_Bhead(o0 - 1), gen_A0(o1), gen_Btail(o0 - 1)] if npre > 0 else [gen_A0(o1)]
            interleave(A1(o0), chain(*first_side), 1)
            interleave(A1(o1), gen_B(o0), 2)
            interleave(gen_C(o0), gen_B(o1), 2)
            interleave(gen_C(o1), chain(gen_Cfin(o0), delay(2), ST(o0), gen_A0(o2)), 1)
            interleave(A1(o2), chain(gen_Cfin(o1), delay(2), ST(o1), gen_A0(o3)), 1)
            interleave(A1(o3), gen_B(o2), 2)
            interleave(gen_C(o2), gen_B(o3), 2)
            interleave(gen_C(o3), chain(gen_Cfin(o2), delay(2), ST(o2)), 1)
            run(gen_Cfin(o3))
            run(ST(o3))

        S.emit(nc, stack, ["xsto0", "xsto1"])
    nc._pe_tags = S.pe_tags
    return nc


def _perm_q():
    p = np.arange(128)
    return np.concatenate([np.where(p < 64, c * 64 + p, (4 + c) * 64 + (p - 64)) for c in range(4)])


def _col(v):
    return np.ascontiguousarray(v.reshape(-1, 128).T)


def prep_shared(inp):
    f = lambda a: np.asarray(a, dtype=np.float32)
    sh = {}
    for i, nm in ((1, "ffn1"), (2, "ffn2")):
        wgu = f(inp[nm + "_w_gu"])[0].reshape(8, 128, 2, NJ, 128)
        sh["wgu%d" % i] = np.ascontiguousarray(wgu.transpose(3, 1, 0, 2, 4)).reshape(NJ, 128, 2048)
        wdn = f(inp[nm + "_w_down"])[0].reshape(NJ, 128, 8, 128)
        sh["wd%d" % i] = np.ascontiguousarray(wdn.transpose(2, 1, 0, 3)).reshape(8, 128, 2816)
    pq = _perm_q()
    colperm = np.concatenate([np.arange(1024), 1024 + pq, np.arange(1536, 1792)])
    win = f(inp["w_in"])[0][:, colperm].reshape(8, 128, 14, 128)
    sh["win"] = np.ascontiguousarray(win.transpose(2, 1, 0, 3)).reshape(14, 128, 1024)
    rowperm = np.concatenate([np.arange(512), 512 + pq])
    wo = f(inp["w_o"])[0][rowperm, :].reshape(8, 128, 8, 128)
    sh["wo"] = np.ascontiguousarray(wo.transpose(2, 1, 0, 3)).reshape(8, 128, 1024)
    vec = np.zeros((128, NV), np.float32)
    for col, nm in ((V_F1PRE, "ffn1_pre_g"), (V_F1POST, "ffn1_post_g"), (V_MIXPRE, "mix_pre_g"),
                    (V_MIXPOST, "mix_post_g"), (V_F2PRE, "ffn2_pre_g"), (V_F2POST, "ffn2_post_g")):
        vec[:, col:col + 8] = _col(f(inp[nm])[0])
    cw = f(inp["conv_w"])[0]
    for c in range(4):
        for k in range(4):
            vec[:, V_CONVW + c * 4 + k] = cw[k, c * 128:(c + 1) * 128]
    vec[:, V_CONVB:V_CONVB + 4] = _col(f(inp["conv_b"])[0])
    vec[:, V_BRG:V_BRG + 4] = _col(f(inp["b_rg"])[0])
    vec[:, V_BIG:V_BIG + 4] = _col(f(inp["b_ig"])[0])
    vec[:, V_LAM:V_LAM + 4] = _col(f(inp["lru_lambda"])[0])
    vec[:, V_GLRU:V_GLRU + 4] = _col(f(inp["g_lru_out"])[0])
    vec[:, V_GATTN:V_GATTN + 4] = _col(f(inp["g_attn_out"])[0][pq])
    sk = f(inp["sinks"])[0]
    for g in range(4):
        vec[0:64, V_SINK + g] = sk[g]
        vec[64:128, V_SINK + g] = sk[4 + g]
    sh["vecs"] = vec
    wg = np.zeros((128, 8, 128), np.float32)
    for gi, nm in enumerate(("w_rg", "w_ig")):
        w = f(inp[nm])[0]
        for c in range(4):
            wg[0:64, gi * 4 + c, 0:64] = w[2 * c]
            wg[64:128, gi * 4 + c, 64:128] = w[2 * c + 1]
    sh["wg"] = wg.reshape(128, 1024)
    sh["ident"] = np.eye(128, dtype=np.float32)
    return sh


def core_inputs(sh, x, core, npre):
    b, r = core // 4, core % 4
    ntiles = npre + NOWN
    xs = np.zeros((ntiles * NT, D), np.float32)
    end = (r + 1) * 2048
    start = max(0, end - ntiles * NT)
    xs[ntiles * NT - (end - start):] = x[b, start:end]
    m = dict(sh)
    m["xs"] = xs
    vec = sh["vecs"].copy()
    t0 = ntiles - (end - start) // NT
    for t in range(ntiles):
        vec[:, V_FLAGS + t] = 1.0 if (t - 1) >= t0 else 0.0
    m["vecs"] = vec
    kj = np.arange(128)[:, None]
    qi = np.arange(128)[None, :]
    cur = (kj <= qi).astype(np.float32)
    prev = (kj > qi).astype(np.float32)
    first = prev * (1.0 if r > 0 else 0.0)
    m["masks"] = np.ascontiguousarray(np.stack([cur, prev, first], axis=1)).reshape(128, 384)
    return m


_NC_CACHE = {}


def kernel(**inputs):
    x = np.asarray(inputs["x"], dtype=np.float32)
    sh = prep_shared(inputs)
    npre = 12
    if "full" not in _NC_CACHE:
        _NC_CACHE["full"] = build_program(npre=npre)
    nc = _NC_CACHE["full"]
    in_maps = [core_inputs(sh, x, c, npre) for c in range(8)]
    res = run_bass_kernel_spmd(nc, in_maps, core_ids=list(range(8)))
    outp = np.empty((2, SEQ, D), np.float32)
    for c in range(8):
        b, r = c // 4, c % 4
        outp[b, r * 2048:(r + 1) * 2048] = res.results[c]["out"]
    return outp
```

```python
import contextlib
import sys
import numpy as np
import concourse.bass as bass
import concourse.mybir as mybir
from concourse.bass_utils import run_bass_kernel_spmd

F32 = mybir.dt.float32
BF16 = mybir.dt.bfloat16
AF = mybir.ActivationFunctionType
ALU = mybir.AluOpType

D = 1024
DFF = 2816
NJ = DFF // 128
SEQ = 8192
NT = 512
NTILES = 16
NOWN = 4
EPS = 1e-6
NSLOT = 5
PIPELINE = True
USE_PRECAST = False
PRECAST_PER_TILE = 5
XQ = "sp"
CONV_ENG = "dve"
SLOT_ELEMS = 2816

V_F1PRE, V_F1POST, V_MIXPRE, V_MIXPOST, V_F2PRE, V_F2POST = 0, 8, 16, 24, 32, 40
V_CONVW, V_CONVB, V_BRG, V_BIG, V_LAM, V_GLRU, V_GATTN, V_SINK, V_FLAGS = 48, 64, 68, 72, 76, 80, 84, 88, 92
NV = 92 + NTILES


class Op:
    __slots__ = ("eng", "fn", "stream", "deps", "sig", "sigval", "waits", "tag")


class Sched:
    def __init__(self):
        self.ops = []
        self.lastw = {}
        self.readers = {}
        self.group_streams = set()

    def add(self, eng, fn, reads=(), writes=(), stream=None):
        op = Op()
        op.eng, op.fn, op.stream = eng, fn, stream
        op.tag = sys._getframe(1).f_code.co_name
        op.sig, op.sigval, op.waits = False, 0, []
        deps = {}
        for k in reads:
            w = self.lastw.get(k)
            if w is not None:
                deps[id(w)] = (w, True)
        for k in writes:
            w = self.lastw.get(k)
            if w is not None and id(w) not in deps:
                deps[id(w)] = (w, False)
            for r in self.readers.get(k, ()):
                if id(r) not in deps and r is not op:
                    deps[id(r)] = (r, False)
        op.deps = []
        for (y, raw) in deps.values():
            if y.stream is None and stream is None and y.eng == eng and not raw:
                continue
            op.deps.append(y)
            y.sig = True
        for k in writes:
            self.lastw[k] = op
            self.readers[k] = []
        for k in reads:
            self.readers.setdefault(k, []).append(op)
        self.ops.append(op)
        return op

    def emit(self, nc, stack, final_streams):
        engs = ["pe", "act", "dve", "pool", "sp"]
        esem = {e: stack.enter_context(nc.semaphore("sem_" + e)) for e in engs}
        ssem = {}
        cnt = {e: 0 for e in engs}
        scnt = {}
        for op in self.ops:
            if op.stream is not None:
                if op.stream not in ssem:
                    ssem[op.stream] = stack.enter_context(nc.semaphore("st_" + op.stream))
                    scnt[op.stream] = 0
                scnt[op.stream] += 16
                op.sigval = scnt[op.stream]
            elif op.sig:
                cnt[op.eng] += 1
                op.sigval = cnt[op.eng]
        for op in self.ops:
            if op.stream in self.group_streams:
                op.sigval = scnt[op.stream]
        waited = {e: {} for e in engs}
        per = {e: [] for e in engs}
        for op in self.ops:
            w = waited[op.eng]
            need = {}
            for y in op.deps:
                s = ssem[y.stream] if y.stream is not None else esem[y.eng]
                key = id(s)
                if y.sigval > w.get(key, 0) and y.sigval > need.get(key, (None, 0))[1]:
                    need[key] = (s, y.sigval)
            for key, (s, v) in need.items():
                w[key] = v
                op.waits.append((s, v))
            per[op.eng].append(op)
        block = stack.enter_context(nc.Block())

        self.pe_tags = []

        class _Cnt:
            def __init__(self, e):
                self.e, self.n = e, 0

            def matmul(self, *a, **k):
                self.n += 1
                return self.e.matmul(*a, **k)

            def transpose(self, *a, **k):
                self.n += 1
                return self.e.transpose(*a, **k)

        def run(e, name):
            ce = _Cnt(e) if name == "pe" else e
            for op in per[name]:
                for (s, v) in op.waits:
                    e.wait_ge(s, v)
                if name == "pe":
                    n0 = ce.n
                ins = op.fn(ce)
                if name == "pe":
                    self.pe_tags.append((op.tag, n0, ce.n))
                if op.stream is not None:
                    ins.then_inc(ssem[op.stream], 16)
                elif op.sig:
                    ins.then_inc(esem[name], 1)
            if name == XQ:
                for st in final_streams:
                    if st in ssem:
                        e.wait_ge(ssem[st], scnt[st])

        block.tensor(lambda e: run(e, "pe"))
        block.scalar(lambda e: run(e, "act"))
        block.vector(lambda e: run(e, "dve"))
        block.gpsimd(lambda e: run(e, "pool"))
        block.sync(lambda e: run(e, "sp"))


def build_program(npre=12, nown=NOWN, stop_after="all"):
    ntiles = npre + nown
    nc = bass.Bass("TRN2", target_bir_lowering=False)
    dt_in = lambda n, s: nc.dram_tensor(n, s, F32, kind="ExternalInput").ap()
    xs = dt_in("xs", [ntiles * NT, D])
    wgu = [dt_in("wgu1", [NJ, 128, 2048]), dt_in("wgu2", [NJ, 128, 2048])]
    wd = [dt_in("wd1", [8, 128, 2816]), dt_in("wd2", [8, 128, 2816])]
    win = dt_in("win", [14, 128, 1024])
    wo = dt_in("wo", [8, 128, 1024])
    vecs_d = dt_in("vecs", [128, NV])
    wg_d = dt_in("wg", [128, 8 * 128])
    masks_d = dt_in("masks", [128, 3 * 128])
    ident_d = dt_in("ident", [128, 128])
    out = nc.dram_tensor("out", [nown * NT, D], F32, kind="ExternalOutput").ap()
    dt_bf = lambda n, s: nc.dram_tensor(n, s, BF16).ap()
    wgu_b = [dt_bf("wgu1_bf", [NJ, 128, 2048]), dt_bf("wgu2_bf", [NJ, 128, 2048])]
    wd_b = [dt_bf("wd1_bf", [8, 128, 2816]), dt_bf("wd2_bf", [8, 128, 2816])]
    win_b = dt_bf("win_bf", [14, 128, 1024])
    wo_b = dt_bf("wo_bf", [8, 128, 1024])

    S = Sched()
    stack = contextlib.ExitStack()
    with stack:
        sb = lambda n, s, d: stack.enter_context(nc.sbuf_tensor("s_" + n, s, d))
        xT = [sb("xT%d" % i, [128, 8, NT], F32) for i in range(2)]
        xst = [sb("xst%d" % i, [128, D], F32) for i in range(2)]
        hnf = [sb("hnf%d" % i, [128, 8, NT], BF16) for i in range(2)]
        hnm = sb("hnm", [128, 8, NT], BF16)
        sq = [sb("sq%d" % i, [128, NT], BF16) for i in range(12)]
        rs = [sb("rs%d" % i, [128, NT], F32) for i in range(2)]
        act = sb("act", [128, NJ, NT], BF16)
        sg = [sb("sg%d" % i, [128, NT], F32) for i in range(2)]
        y = sb("y", [128, 8, NT], F32)
        ring = [sb("ring%d" % i, [128, SLOT_ELEMS], BF16) for i in range(NSLOT)]
        xl = sb("xl", [128, 4, NT + 3], F32)
        xc = sb("xc", [128, NT], F32)
        xcb = sb("xcb", [128, NT], BF16)
        r_t = sb("r_t", [128, NT], F32)
        i_t = sb("i_t", [128, NT], F32)
        a_t = sb("a_t", [128, NT], F32)
        m_t = sb("m_t", [128, NT], F32)
        hseq = sb("hseq", [128, NT], F32)
        hstate = sb("hstate", [128, 4], F32)
        hsin = sb("hsin", [128, 4], F32)
        gl = sb("gl", [128, NT], F32)
        ycat = sb("ycat", [128, 8, NT], F32)
        qT = sb("qT", [128, 4, NT], BF16)
        kT = sb("kT", [128, 128 + NT], BF16)
        vtok = sb("vtok", [128, 5, 128], BF16)
        pbuf = [sb("p%d" % i, [128, NT], BF16) for i in range(4)]
        dn = sb("dn", [128, NT], F32)
        vec = sb("vec", [128, NV], F32)
        gh = sb("gh", [128, 24], F32)
        c8 = sb("c8", [128, 4], F32)
        epsb = sb("epsb", [128, 1], F32)
        oneb = sb("oneb", [128, 1], F32)
        tmp4 = [sb("tmp4_%d" % i, [128, 4], F32) for i in range(4)]
        esink = sb("esink", [128, 4], F32)
        wgb = sb("wgb", [128, 8, 128], BF16)
        masks = sb("masks", [128, 3, 128], BF16)
        ident = sb("ident", [128, 128], F32)
        ones = sb("ones", [128, 128], BF16)
        ps = [stack.enter_context(nc.psum_tensor("ps%d" % i, [128, NT], F32)) for i in range(8)]

        rr = {"bank": 0, "slot": 0, "sq": 0, "rs": 0, "sg": 0, "p": 0, "sqm": 0, "sqs": 0}
        sq_alloc = {"n": lambda: nxt("sq", 8), "m": lambda: 8 + nxt("sqm", 2), "s": lambda: 10 + nxt("sqs", 2)}

        def nxt(name, n):
            v = rr[name]
            rr[name] = (v + 1) % n
            return v

        rr["mbank"] = 0
        bank_mode = {"split": False}
        rr["sbank"] = 0

        def nb():
            return 4 + nxt("sbank", 3) if bank_mode["split"] else nxt("bank", 7)

        def nbm():
            return nxt("mbank", 4) if bank_mode["split"] else nxt("bank", 7)

        SSB1 = 7

        S.add("sp", lambda e: e.dma_start(out=vec[:], in_=vecs_d[:, :]), writes=["vec"], stream="c_vec")
        S.add("sp", lambda e: e.dma_start(out=ident[:], in_=ident_d[:, :]), writes=["ident"], stream="c_id")
        S.add("pool", lambda e: e.dma_start(out=wgb[:].rearrange("p a b -> p (a b)"), in_=wg_d[:, :]),
              writes=["wgb"], stream="c_wg")
        S.add("pool", lambda e: e.dma_start(out=masks[:].rearrange("p a b -> p (a b)"), in_=masks_d[:, :]),
              writes=["masks"], stream="c_mask")
        S.add("dve", lambda e: e.memset(ones[:], 1.0), writes=["ones"])
        S.add("dve", lambda e: e.memset(epsb[:], EPS), writes=["epsb"])
        S.add("dve", lambda e: e.memset(oneb[:], 1.0), writes=["oneb"])
        S.add("dve", lambda e: e.memset(xl[:].rearrange("p a b -> p (a b)"), 0.0), writes=[("xl", c) for c in range(4)])
        S.add("dve", lambda e: e.memset(hstate[:], 0.0), writes=["hstate"])
        S.add("dve", lambda e: e.memset(kT[:], 0.0), writes=["kT"])
        S.add("dve", lambda e: e.memset(vtok[:].rearrange("p a b -> p (a b)"), 0.0), writes=["vtok"])
        S.add("dve", lambda e: e.tensor_scalar(out=gh[:, 0:8], in0=vec[:, V_F1POST:V_F1POST + 8], scalar1=0.5,
                                               scalar2=None, op0=ALU.mult), reads=["vec"], writes=["gh0"])
        S.add("dve", lambda e: e.tensor_copy(out=gh[:, 8:16], in_=vec[:, V_MIXPOST:V_MIXPOST + 8]),
              reads=["vec"], writes=["gh1"])
        S.add("dve", lambda e: e.tensor_scalar(out=gh[:, 16:24], in0=vec[:, V_F2POST:V_F2POST + 8], scalar1=0.5,
                                               scalar2=None, op0=ALU.mult), reads=["vec"], writes=["gh2"])
        t0_, t1_, t2_, t3_ = tmp4
        lam = vec[:, V_LAM:V_LAM + 4]
        S.add("act", lambda e: e.activation(out=t0_[:], in_=lam, func=AF.Exp, scale=-1.0), reads=["vec"], writes=["t0"])
        S.add("dve", lambda e: e.tensor_scalar(out=t1_[:], in0=t0_[:], scalar1=1.0, scalar2=None, op0=ALU.add),
              reads=["t0"], writes=["t1"])
        S.add("act", lambda e: e.activation(out=t2_[:], in_=t1_[:], func=AF.Ln), reads=["t1"], writes=["t2"])
        S.add("dve", lambda e: e.tensor_scalar(out=t3_[:], in0=t1_[:], scalar1=-1.0, scalar2=None, op0=ALU.add),
              reads=["t1"], writes=["t3"])
        S.add("dve", lambda e: e.reciprocal(out=t3_[:], in_=t3_[:]), reads=["t3"], writes=["t3"])
        S.add("dve", lambda e: e.tensor_tensor(out=t3_[:], in0=t3_[:], in1=t0_[:], op=ALU.mult),
              reads=["t3", "t0"], writes=["t3"])
        S.add("dve", lambda e: e.tensor_tensor(out=t3_[:], in0=t3_[:], in1=t2_[:], op=ALU.mult),
              reads=["t3", "t2"], writes=["t3"])
        S.add("dve", lambda e: e.tensor_scalar(out=c8[:], in0=t3_[:], scalar1=-8.0, scalar2=None, op0=ALU.mult),
              reads=["t3"], writes=["c8"])
        S.add("act", lambda e: e.activation(out=esink[:], in_=vec[:, V_SINK:V_SINK + 4], func=AF.Exp),
              reads=["vec"], writes=["esink"])

        precast_done = set()
        bf_ok = set()

        def precast(name, j, group):
            src32, srcbf = WSRC[name]
            S.group_streams.add(group)
            S.add("pool", lambda e: e.dma_start(out=srcbf[j], in_=src32[j], max_dma_last_dim=4096),
                  writes=[("wb", name, j)], stream=group)
            precast_done.add((name, j))

        def wload(name, j, nel):
            src32, srcbf = WSRC[name]
            slot = nxt("slot", NSLOT)
            if (name, j) in bf_ok:
                S.add("pool", lambda e: e.dma_start(out=ring[slot][:, 0:nel], in_=srcbf[j]),
                      reads=[("wb", name, j)], writes=[("ring", slot)], stream="ringS%d" % slot)
            else:
                S.add("pool", lambda e: e.dma_start(out=ring[slot][:, 0:nel], in_=src32[j], max_dma_last_dim=4096),
                      writes=[("ring", slot)], stream="ringS%d" % slot)
            return slot

        WSRC = {"gu0": (wgu[0], wgu_b[0]), "gu1": (wgu[1], wgu_b[1]), "d0": (wd[0], wd_b[0]), "d1": (wd[1], wd_b[1]),
                "win": (win, win_b), "wo": (wo, wo_b)}

        def sumsq_squares(src_fn, src_keys, nchunks):
            sis = []
            for k in range(nchunks):
                si = sq_alloc["n"]()
                sis.append(si)
                S.add("act", lambda e, k=k, si=si: e.activation(out=sq[si][:], in_=src_fn(k), func=AF.Square),
                      reads=[src_keys[k]], writes=[("sq", si)])
            return sis

        def sumsq_finish(sis, inv_n):
            bk = nb()
            n = len(sis)
            for k, si in enumerate(sis):
                S.add("pe", lambda e, k=k, si=si: e.matmul(ps[bk][:], lhsT=ones[:], rhs=sq[si][:], start=(k == 0),
                                                          stop=(k == n - 1)),
                      reads=[("sq", si), "ones"], writes=[("ps", bk)])
            return finish_rstd(bk, inv_n)

        def finish_rstd(bk, inv_n):
            ri = nxt("rs", 2)
            S.add("act", lambda e: e.activation(out=rs[ri][:], in_=ps[bk][:], func=AF.Ln, scale=inv_n, bias=epsb[:, 0:1]),
                  reads=[("ps", bk), "epsb"], writes=[("rs", ri)])
            S.add("act", lambda e: e.activation(out=rs[ri][:], in_=rs[ri][:], func=AF.Exp, scale=-0.5),
                  reads=[("rs", ri)], writes=[("rs", ri)])
            return ri

        def gen_prenorm(par, gcol, dst, dkey, split=True):
            xt = xT[par]
            if split:
                sis = sumsq_squares(lambda k: xt[:, k, :], [("xT", par, k) for k in range(8)], 8)
                yield
                yield
                ri = sumsq_finish(sis, 1.0 / D)
            else:
                bk = nbm()
                for k in range(8):
                    si = sq_alloc["m"]()
                    S.add("act", lambda e, k=k, si=si: e.activation(out=sq[si][:], in_=xt[:, k, :], func=AF.Square),
                          reads=[("xT", par, k)], writes=[("sq", si)])
                    S.add("pe", lambda e, k=k, si=si: e.matmul(ps[bk][:], lhsT=ones[:], rhs=sq[si][:], start=(k == 0),
                                                              stop=(k == 7)),
                          reads=[("sq", si), "ones"], writes=[("ps", bk)])
                ri = finish_rstd(bk, 1.0 / D)
            for k in range(8):
                S.add("dve", lambda e, k=k: e.scalar_tensor_tensor(out=dst[:, k, :], in0=xt[:, k, :],
                                                                  scalar=vec[:, gcol + k:gcol + k + 1], in1=rs[ri][:],
                                                                  op0=ALU.mult, op1=ALU.mult),
                      reads=[("xT", par, k), ("rs", ri), "vec"], writes=[dkey(k)])
            yield

        def gen_outproj(group_fn, ybuf, ykey, ssb, pool):
            pend = None
            for m in range(8):
                bk = group_fn(m)
                si = sq_alloc[pool]()
                S.add("act", lambda e, m=m, bk=bk: e.activation(out=ybuf[:, m, :], in_=ps[bk][:], func=AF.Copy),
                      reads=[("ps", bk)], writes=[(ykey, m)])
                S.add("act", lambda e, bk=bk, si=si: e.activation(out=sq[si][:], in_=ps[bk][:], func=AF.Square),
                      reads=[("ps", bk)], writes=[("sq", si)])
                if pend is not None:
                    pm, psi = pend
                    S.add("pe", lambda e, pm=pm, psi=psi: e.matmul(ps[ssb][:], lhsT=ones[:], rhs=sq[psi][:],
                                                                  start=(pm == 0), stop=False),
                          reads=[("sq", psi), "ones"], writes=[("ps", ssb)])
                pend = (m, si)
                yield
            pm, psi = pend
            S.add("pe", lambda e: e.matmul(ps[ssb][:], lhsT=ones[:], rhs=sq[psi][:], start=False, stop=True),
                  reads=[("sq", psi), "ones"], writes=[("ps", ssb)])
            yield

        def gen_residual(par, ybuf, ykey, ssb, ghcol, ghkey, ri=None):
            xt = xT[par]
            if ri is None:
                ri = finish_rstd(ssb, 1.0 / D)
            for k in range(8):
                S.add("dve", lambda e, k=k: e.scalar_tensor_tensor(out=ybuf[:, k, :], in0=ybuf[:, k, :],
                                                                  scalar=gh[:, ghcol + k:ghcol + k + 1], in1=rs[ri][:],
                                                                  op0=ALU.mult, op1=ALU.mult),
                      reads=[(ykey, k), ("rs", ri), ghkey], writes=[(ykey, k)])
            yield
            for k in range(8):
                S.add("dve", lambda e, k=k: e.tensor_tensor(out=xt[:, k, :], in0=xt[:, k, :], in1=ybuf[:, k, :], op=ALU.add),
                      reads=[(ykey, k), ("xT", par, k)], writes=[("xT", par, k)])
            yield

        def gen_ffn_main(which, par):
            hn = hnf[par]
            hk = [("hnf", par, k) for k in range(8)]
            for j in range(NJ):
                slot = wload("gu%d" % which, j, 2048)
                bg = nbm()
                bu = nbm()

                def mm(e, slot=slot, bk=bg, off=0):
                    for k in range(8):
                        ins = e.matmul(ps[bk][:], lhsT=ring[slot][:, k * 256 + off:k * 256 + off + 128], rhs=hn[:, k, :],
                                       start=(k == 0), stop=(k == 7))
                    return ins
                S.add("pe", mm, reads=[("ring", slot)] + hk, writes=[("ps", bg)])
                S.add("pe", lambda e, mm=mm, slot=slot, bu=bu: mm(e, slot, bu, 128), reads=[("ring", slot)] + hk,
                      writes=[("ps", bu)])
                gi = nxt("sg", 2)
                S.add("act", lambda e, bg=bg, gi=gi: e.activation(out=sg[gi][:], in_=ps[bg][:], func=AF.Silu),
                      reads=[("ps", bg)], writes=[("sg", gi)])
                S.add("dve", lambda e, bu=bu, gi=gi, j=j: e.tensor_tensor(out=act[:, j, :], in0=ps[bu][:], in1=sg[gi][:],
                                                                         op=ALU.mult),
                      reads=[("ps", bu), ("sg", gi)], writes=[("act", j)])
                yield

            def down(m):
                slot = wload("d%d" % which, m, 2816)
                bk = nbm()

                def mm(e, lo=0, hi=NJ):
                    for kk in range(lo, hi):
                        ins = e.matmul(ps[bk][:], lhsT=ring[slot][:, kk * 128:(kk + 1) * 128], rhs=act[:, kk, :],
                                       start=(kk == 0), stop=(kk == NJ - 1))
                    return ins
                sp = NJ - 4
                S.add("pe", lambda e: mm(e, 0, sp), reads=[("ring", slot)] + [("act", kk) for kk in range(sp)],
                      writes=[("ps", bk)])
                S.add("pe", lambda e: mm(e, sp, NJ), reads=[("ring", slot)] + [("act", kk) for kk in range(sp, NJ)],
                      writes=[("ps", bk)])
                return bk
            yield from gen_outproj(down, y, "y", SSB1, "m")

        def proj_chunk(n):
            slot = wload("win", n, 1024)
            bk = nb()

            def mm(e):
                for k in range(8):
                    ins = e.matmul(ps[bk][:], lhsT=ring[slot][:, k * 128:(k + 1) * 128], rhs=hnm[:, k, :],
                                   start=(k == 0), stop=(k == 7))
                return ins
            S.add("pe", mm, reads=[("ring", slot)] + [("hnm", k) for k in range(8)], writes=[("ps", bk)])
            return bk

        def gen_lru(t, own):
            fl = vec[:, V_FLAGS + t:V_FLAGS + t + 1]
            for c in range(4):
                bk = proj_chunk(c)
                S.add("dve", lambda e, c=c: e.tensor_scalar(out=xl[:, c, 0:3], in0=xl[:, c, NT:NT + 3], scalar1=fl,
                                                            scalar2=None, op0=ALU.mult),
                      reads=[("xl", c), "vec"], writes=[("xl", c)])
                S.add("act", lambda e, c=c, bk=bk: e.activation(out=xl[:, c, 3:NT + 3], in_=ps[bk][:], func=AF.Copy),
                      reads=[("ps", bk)], writes=[("xl", c)])
                if c % 2 == 1:
                    yield
            for c in range(4):
                yield from gen_lru_chunk(t, c, own, fl)

        def gen_lru_chunk(t, c, own, fl):
            cw = lambda k: vec[:, V_CONVW + c * 4 + k:V_CONVW + c * 4 + k + 1]
            S.add(CONV_ENG, lambda e: e.tensor_scalar(out=xc[:], in0=xl[:, c, 3:NT + 3], scalar1=cw(3),
                                                   scalar2=vec[:, V_CONVB + c:V_CONVB + c + 1], op0=ALU.mult, op1=ALU.add),
                  reads=[("xl", c), "vec"], writes=["xc"])
            for k in range(3):
                if CONV_ENG == "dve":
                    S.add("dve", lambda e, k=k: e.scalar_tensor_tensor(out=xc[:], in0=xl[:, c, k:k + NT], scalar=cw(k),
                                                                      in1=xc[:], op0=ALU.mult, op1=ALU.add),
                          reads=[("xl", c), "vec", "xc"], writes=["xc"])
                else:
                    S.add("pool", lambda e, k=k: e.tensor_scalar(out=a_t[:], in0=xl[:, c, k:k + NT], scalar1=cw(k),
                                                                 scalar2=0.0, op0=ALU.mult, op1=ALU.add),
                          reads=[("xl", c), "vec"], writes=["a_t"])
                    S.add("pool", lambda e: e.tensor_tensor(out=xc[:], in0=xc[:], in1=a_t[:], op=ALU.add),
                          reads=["xc", "a_t"], writes=["xc"])
            S.add("dve", lambda e: e.tensor_copy(out=xcb[:], in_=xc[:]), reads=["xc"], writes=["xcb"])
            if own:
                bgk = proj_chunk(4 + c)
                S.add("act", lambda e: e.activation(out=gl[:], in_=ps[bgk][:], func=AF.Gelu),
                      reads=[("ps", bgk)], writes=["gl"])
            yield
            yield
            yield
            br = nb()
            bi = nb()
            S.add("pe", lambda e: e.matmul(ps[br][:], lhsT=wgb[:, c, :], rhs=xcb[:], start=True, stop=True),
                  reads=["xcb", "wgb"], writes=[("ps", br)])
            S.add("pe", lambda e: e.matmul(ps[bi][:], lhsT=wgb[:, 4 + c, :], rhs=xcb[:], start=True, stop=True),
                  reads=["xcb", "wgb"], writes=[("ps", bi)])
            S.add("act", lambda e: e.activation(out=r_t[:], in_=ps[br][:], func=AF.Sigmoid,
                                                bias=vec[:, V_BRG + c:V_BRG + c + 1]),
                  reads=[("ps", br), "vec"], writes=["r_t"])
            S.add("act", lambda e: e.activation(out=i_t[:], in_=ps[bi][:], func=AF.Sigmoid,
                                                bias=vec[:, V_BIG + c:V_BIG + c + 1]),
                  reads=[("ps", bi), "vec"], writes=["i_t"])
            S.add("act", lambda e: e.activation(out=a_t[:], in_=r_t[:], func=AF.Exp, scale=c8[:, c:c + 1]),
                  reads=["r_t", "c8"], writes=["a_t"])
            S.add("act", lambda e: e.activation(out=m_t[:], in_=a_t[:], func=AF.Square), reads=["a_t"], writes=["m_t"])
            S.add("act", lambda e: e.activation(out=m_t[:], in_=m_t[:], func=AF.Ln, scale=-1.0, bias=oneb[:, 0:1]),
                  reads=["m_t", "oneb"], writes=["m_t"])
            S.add("act", lambda e: e.activation(out=m_t[:], in_=m_t[:], func=AF.Exp, scale=0.5),
                  reads=["m_t"], writes=["m_t"])
            S.add("dve", lambda e: e.tensor_tensor(out=i_t[:], in0=i_t[:], in1=xc[:], op=ALU.mult),
                  reads=["i_t", "xc"], writes=["i_t"])
            S.add("dve", lambda e: e.tensor_tensor(out=i_t[:], in0=i_t[:], in1=m_t[:], op=ALU.mult),
                  reads=["i_t", "m_t"], writes=["i_t"])
            S.add("dve", lambda e: e.tensor_scalar(out=hsin[:, c:c + 1], in0=hstate[:, c:c + 1], scalar1=fl, scalar2=None,
                                                   op0=ALU.mult), reads=["hstate", "vec"], writes=["hsin"])
            S.add("dve", lambda e: e.tensor_tensor_scan(out=hseq[:], data0=a_t[:], data1=i_t[:], initial=hsin[:, c:c + 1],
                                                        op0=ALU.mult, op1=ALU.add),
                  reads=["a_t", "i_t", "hsin"], writes=["hseq"])
            S.add("dve", lambda e: e.tensor_copy(out=hstate[:, c:c + 1], in_=hseq[:, NT - 1:NT]),
                  reads=["hseq"], writes=["hstate"])
            if own:
                S.add("dve", lambda e: e.tensor_tensor(out=ycat[:, c, :], in0=hseq[:], in1=gl[:], op=ALU.mult),
                      reads=["hseq", "gl"], writes=[("ycat", c)])
            yield

        def gen_kv_proj():
            bk = proj_chunk(12)
            S.add("dve", lambda e: e.tensor_copy(out=kT[:, 0:128], in_=kT[:, NT:NT + 128]), reads=["kT"], writes=["kT"])
            S.add("act", lambda e: e.activation(out=kT[:, 128:128 + NT], in_=ps[bk][:], func=AF.Copy),
                  reads=[("ps", bk)], writes=["kT"])
            yield
            slot = wload("win", 13, 1024)
            bv = nb()

            def mm(e):
                for b in range(4):
                    for k in range(8):
                        ins = e.matmul(ps[bv][:, b * 128:(b + 1) * 128], lhsT=hnm[:, k, b * 128:(b + 1) * 128],
                                       rhs=ring[slot][:, k * 128:(k + 1) * 128], start=(k == 0), stop=(k == 7))
                return ins
            S.add("pe", mm, reads=[("ring", slot)] + [("hnm", k) for k in range(8)], writes=[("ps", bv)])
            S.add("dve", lambda e: e.tensor_copy(out=vtok[:, 0, :], in_=vtok[:, 4, :]), reads=["vtok"], writes=["vtok"])
            S.add("act", lambda e: e.activation(out=vtok[:, 1:5, :], in_=ps[bv][:].rearrange("p (a b) -> p a b", a=4),
                                                func=AF.Copy), reads=[("ps", bv)], writes=["vtok"])
            yield

        def gen_attention(first_own):
            for c in range(4):
                bk = proj_chunk(8 + c)
                S.add("act", lambda e, c=c, bk=bk: e.activation(out=qT[:, c, :], in_=ps[bk][:], func=AF.Copy, scale=0.125),
                      reads=[("ps", bk)], writes=["qT"])
                if c % 2 == 1:
                    yield
            yield from gen_kv_proj()
            yield
            for qb in range(4):
                yield from gen_attn_block(qb, first_own and qb == 0)

        def gen_attn_block(qb, use_first_mask):
            v3 = lambda ap: ap.rearrange("p (a b) -> p a b", a=4)
            pis = {}
            for kv in range(2):
                pr = slice(kv * 64, (kv + 1) * 64)
                for kb in range(2):
                    bk = nb()
                    kc = (qb + kb) * 128
                    S.add("pe", lambda e, bk=bk, pr=pr, kc=kc: e.matmul(
                        ps[bk][:], lhsT=kT[pr, kc:kc + 128], rhs=qT[pr, :, qb * 128:(qb + 1) * 128],
                        start=True, stop=True), reads=["kT", "qT"], writes=[("ps", bk)])
                    pi = nxt("p", 4)
                    pis[(kv, kb)] = pi
                    S.add("act", lambda e, bk=bk, pi=pi: e.activation(out=pbuf[pi][:], in_=ps[bk][:], func=AF.Exp),
                          reads=[("ps", bk)], writes=[("p", pi)])
                    mi = 0 if kb == 1 else (2 if use_first_mask else 1)
                    S.add("dve", lambda e, pi=pi, mi=mi: e.tensor_tensor(
                        out=v3(pbuf[pi][:]), in0=v3(pbuf[pi][:]),
                        in1=masks[:, mi, :].unsqueeze(1).to_broadcast([128, 4, 128]), op=ALU.mult),
                        reads=[("p", pi), "masks"], writes=[("p", pi)])
                yield
            yield
            ob = nb()
            db = nb()
            for kv in range(2):
                pr = slice(kv * 64, (kv + 1) * 64)

                def mm(e, kv=kv, pr=pr):
                    for kb in range(2):
                        e.matmul(ps[ob][pr, :], lhsT=vtok[:, qb + kb, kv * 64:(kv + 1) * 64],
                                 rhs=pbuf[pis[(kv, kb)]][:], start=(kb == 0), stop=(kb == 1))
                    for kb in range(2):
                        ins = e.matmul(ps[db][pr, :], lhsT=ones[:, 0:64], rhs=pbuf[pis[(kv, kb)]][:],
                                       start=(kb == 0), stop=(kb == 1))
                    return ins
                S.add("pe", mm, reads=["vtok", "ones", ("p", pis[(kv, 0)]), ("p", pis[(kv, 1)])],
                      writes=[("ps", ob), ("ps", db)])
            ysl = lambda: ycat[:, 4:8, qb * 128:(qb + 1) * 128]
            S.add("act", lambda e: e.activation(out=ysl(), in_=v3(ps[ob][:]), func=AF.Copy),
                  reads=[("ps", ob)], writes=[("ycat", 4 + c) for c in range(4)])
            S.add("dve", lambda e: e.tensor_tensor(out=v3(dn[:]), in0=v3(ps[db][:]),
                                                   in1=esink[:, 0:4].unsqueeze(2).to_broadcast([128, 4, 128]),
                                                   op=ALU.add), reads=[("ps", db), "esink"], writes=["dn"])
            S.add("dve", lambda e: e.reciprocal(out=dn[:], in_=dn[:]), reads=["dn"], writes=["dn"])
            S.add("dve", lambda e: e.tensor_tensor(out=ysl(), in0=ysl(), in1=v3(dn[:]), op=ALU.mult),
                  reads=[("ycat", 4 + c) for c in range(4)] + ["dn"], writes=[("ycat", 4 + c) for c in range(4)])
            yield

        def gen_mixer_tail(par):
            for half, gcol in ((0, V_GLRU), (1, V_GATTN)):
                sis = sumsq_squares(lambda k, half=half: ycat[:, half * 4 + k, :],
                                    [("ycat", half * 4 + k) for k in range(4)], 4)
                yield
                yield
                ri = sumsq_finish(sis, 1.0 / 512)
                for k in range(4):
                    kk = half * 4 + k
                    S.add("dve", lambda e, kk=kk, k=k, gcol=gcol, ri=ri: e.scalar_tensor_tensor(
                        out=hnm[:, kk, :], in0=ycat[:, kk, :], scalar=vec[:, gcol + k:gcol + k + 1], in1=rs[ri][:],
                        op0=ALU.mult, op1=ALU.mult), reads=[("ycat", kk), ("rs", ri), "vec"], writes=[("hnm", kk)])
                yield

            def oproj(m):
                slot = wload("wo", m, 1024)
                bk = nb()

                def mm(e):
                    for kk in range(8):
                        ins = e.matmul(ps[bk][:], lhsT=ring[slot][:, kk * 128:(kk + 1) * 128], rhs=hnm[:, kk, :],
                                       start=(kk == 0), stop=(kk == 7))
                    return ins
                S.add("pe", mm, reads=[("ring", slot)] + [("hnm", k) for k in range(8)], writes=[("ps", bk)])
                return bk
            yield
            for m in range(8):
                bk = oproj(m)
                S.add("act", lambda e, m=m, bk=bk: e.activation(out=ycat[:, m, :], in_=ps[bk][:], func=AF.Copy),
                      reads=[("ps", bk)], writes=[("ycat", m)])
                if m % 2 == 1:
                    yield
            sis = sumsq_squares(lambda k: ycat[:, k, :], [("ycat", k) for k in range(8)], 8)
            yield
            yield
            ri = sumsq_finish(sis, 1.0 / D)
            yield from gen_residual(par, ycat, "ycat", None, 8, "gh1", ri=ri)

        def gen_load_x(t, par):
            xt = xT[par]

            def dma(b):
                xi = b % 2
                r0 = t * NT + b * 128
                S.add("sp", lambda e: e.dma_start(out=xst[xi][:], in_=xs[r0:r0 + 128, :]),
                      writes=[("xst", xi)], stream="xld%d" % xi)

            def tr_block(b):
                xi = b % 2
                for half in range(2):
                    bk = nb()

                    def tr(e, bk=bk, half=half):
                        for kk in range(4):
                            k = half * 4 + kk
                            ins = e.transpose(out=ps[bk][:, kk * 128:(kk + 1) * 128], in_=xst[xi][:, k * 128:(k + 1) * 128],
                                              identity=ident[:])
                        return ins
                    S.add("pe", tr, reads=[("xst", xi), "ident"], writes=[("ps", bk)])
                    S.add("act", lambda e, bk=bk, half=half: e.activation(
                        out=xt[:, half * 4:(half + 1) * 4, b * 128:(b + 1) * 128],
                        in_=ps[bk][:].rearrange("p (a b) -> p a b", a=4), func=AF.Copy),
                        reads=[("ps", bk)], writes=[("xT", par, half * 4 + kk) for kk in range(4)])
            dma(0)
            dma(1)
            yield
            for b in range(4):
                tr_block(b)
                if b + 2 < 4:
                    dma(b + 2)
                yield

        def gen_store_x(ot, par):
            xt = xT[par]
            for b in range(4):
                xi = b % 2
                bks = [nb(), nb()]

                def tr(e, bks=bks, b=b):
                    for k in range(8):
                        ins = e.transpose(out=ps[bks[k // 4]][:, (k % 4) * 128:(k % 4 + 1) * 128],
                                          in_=xt[:, k, b * 128:(b + 1) * 128], identity=ident[:])
                    return ins
                S.add("pe", tr, reads=[("xT", par, k) for k in range(8)] + ["ident"],
                      writes=[("ps", bks[0]), ("ps", bks[1])])
                for half in range(2):
                    S.add("act", lambda e, xi=xi, half=half, bks=bks: e.activation(
                        out=xst[xi][:, half * 512:(half + 1) * 512], in_=ps[bks[half]][:], func=AF.Copy),
                        reads=[("ps", bks[half])], writes=[("xst", xi)])
                r0 = ot * NT + b * 128
                S.add(XQ, lambda e, xi=xi, r0=r0: e.dma_start(out=out[r0:r0 + 128, :], in_=xst[xi][:]),
                      reads=[("xst", xi)], stream="xsto%d" % xi)
                yield

        def gen_A0(t):
            par = t % 2
            yield from gen_load_x(t, par)
            yield from gen_prenorm(par, V_F1PRE, hnf[par], lambda k: ("hnf", par, k))

        def roundrobin(g1, g2):
            live = [g1, g2]
            while live:
                for g in list(live):
                    try:
                        next(g)
                        yield
                    except StopIteration:
                        live.remove(g)

        def gen_Bhead(t):
            par = t % 2
            yield from gen_residual(par, y, "y", SSB1, 0, "gh0")
            if t >= npre and stop_after == "ffn1":
                return
            yield from gen_prenorm(par, V_MIXPRE, hnm, lambda k: ("hnm", k))

        def gen_Btail(t):
            par = t % 2
            own = t >= npre
            if own and stop_after == "ffn1":
                return
            if own:
                yield from roundrobin(gen_lru(t, own), gen_attention(first_own=(t == npre)))
                yield from gen_mixer_tail(par)
            else:
                yield from gen_lru(t, own)
            if (not own) and t == npre - 1:
                yield from gen_kv_proj()

        def gen_B(t):
            yield from gen_Bhead(t)
            yield
            yield from gen_Btail(t)
            if t >= npre and stop_after == "all":
                par = t % 2
                yield from gen_prenorm(par, V_F2PRE, hnf[par], lambda k: ("hnf", par, k))

        def gen_C(t):
            par = t % 2
            if not (PIPELINE and stop_after == "all" and nown == 4):
                yield from gen_prenorm(par, V_F2PRE, hnf[par], lambda k: ("hnf", par, k), split=False)
            yield from gen_ffn_main(1, par)

        def gen_Cfin(t):
            yield from gen_residual(t % 2, y, "y", SSB1, 16, "gh2")

        def run(g):
            for _ in g:
                pass

        def delay(n):
            for _ in range(n):
                yield

        def chain(*gs):
            for g in gs:
                yield from g

        def interleave(main, side, ratio=1):
            for _ in main:
                for _ in range(ratio):
                    next(side, None)
            run(side)

        full = stop_after == "all"
        if not (PIPELINE and full and nown == 4):
            for t in range(ntiles):
                run(gen_A0(t))
                run(gen_ffn_main(0, t % 2))
                run(gen_B(t))
                if t >= npre:
                    if full:
                        run(gen_C(t))
                        run(gen_Cfin(t))
                    run(gen_store_x(t - npre, t % 2))
        else:
            A1 = lambda t: gen_ffn_main(0, t % 2)
            ST = lambda t: gen_store_x(t - npre, t % 2)
            run(gen_A0(0))
            pc_plan = ([("gu1", j) for j in range(NJ)] + [("d1", m) for m in range(8)] + [("wo", m) for m in range(8)]
                       + [("win", n) for n in range(4, 14)])
            if npre < 2 or not USE_PRECAST:
                pc_plan = []
            per_tile = -(-len(pc_plan) // max(1, npre - 1))
            for t in range(npre):
                if USE_PRECAST and t >= 1:
                    for _ in range(per_tile):
                        if pc_plan:
                            precast(*pc_plan.pop(0), "pc_own")
                side = []
                if t >= 1:
                    side.append(gen_Bhead(t - 1))
                side.append(gen_A0(t + 1))
                if t >= 1:
                    side.append(gen_Btail(t - 1))
                interleave(A1(t), chain(*side), 1)
            o0, o1, o2, o3 = range(npre, npre + 4)
            if not pc_plan:
                bf_ok.update(precast_done)
            bank_mode["split"] = False
            first_side = [gen_Bhead(o0 - 1), gen_A0(o1), gen_Btail(o0 - 1)] if npre > 0 else [gen_A0(o1)]
            interleave(A1(o0), chain(*first_side), 1)
            interleave(A1(o1), gen_B(o0), 2)
            interleave(gen_C(o0), gen_B(o1), 2)
            interleave(gen_C(o1), chain(gen_Cfin(o0), delay(2), ST(o0), gen_A0(o2)), 1)
            interleave(A1(o2), chain(gen_Cfin(o1), delay(2), ST(o1), gen_A0(o3)), 1)
            interleave(A1(o3), gen_B(o2), 2)
            interleave(gen_C(o2), gen_B(o3), 2)
            interleave(gen_C(o3), chain(gen_Cfin(o2), delay(2), ST(o2)), 1)
            run(gen_Cfin(o3))
            run(ST(o3))

        S.emit(nc, stack, ["xsto0", "xsto1"])
    nc._pe_tags = S.pe_tags
    return nc


def _perm_q():
    p = np.arange(128)
    return np.concatenate([np.where(p < 64, c * 64 + p, (4 + c) * 64 + (p - 64)) for c in range(4)])


def _col(v):
    return np.ascontiguousarray(v.reshape(-1, 128).T)


def prep_shared(inp):
    f = lambda a: np.asarray(a, dtype=np.float32)
    sh = {}
    for i, nm in ((1, "ffn1"), (2, "ffn2")):
        wgu = f(inp[nm + "_w_gu"])[0].reshape(8, 128, 2, NJ, 128)
        sh["wgu%d" % i] = np.ascontiguousarray(wgu.transpose(3, 1, 0, 2, 4)).reshape(NJ, 128, 2048)
        wdn = f(inp[nm + "_w_down"])[0].reshape(NJ, 128, 8, 128)
        sh["wd%d" % i] = np.ascontiguousarray(wdn.transpose(2, 1, 0, 3)).reshape(8, 128, 2816)
    pq = _perm_q()
    colperm = np.concatenate([np.arange(1024), 1024 + pq, np.arange(1536, 1792)])
    win = f(inp["w_in"])[0][:, colperm].reshape(8, 128, 14, 128)
    sh["win"] = np.ascontiguousarray(win.transpose(2, 1, 0, 3)).reshape(14, 128, 1024)
    rowperm = np.concatenate([np.arange(512), 512 + pq])
    wo = f(inp["w_o"])[0][rowperm, :].reshape(8, 128, 8, 128)
    sh["wo"] = np.ascontiguousarray(wo.transpose(2, 1, 0, 3)).reshape(8, 128, 1024)
    vec = np.zeros((128, NV), np.float32)
    for col, nm in ((V_F1PRE, "ffn1_pre_g"), (V_F1POST, "ffn1_post_g"), (V_MIXPRE, "mix_pre_g"),
                    (V_MIXPOST, "mix_post_g"), (V_F2PRE, "ffn2_pre_g"), (V_F2POST, "ffn2_post_g")):
        vec[:, col:col + 8] = _col(f(inp[nm])[0])
    cw = f(inp["conv_w"])[0]
    for c in range(4):
        for k in range(4):
            vec[:, V_CONVW + c * 4 + k] = cw[k, c * 128:(c + 1) * 128]
    vec[:, V_CONVB:V_CONVB + 4] = _col(f(inp["conv_b"])[0])
    vec[:, V_BRG:V_BRG + 4] = _col(f(inp["b_rg"])[0])
    vec[:, V_BIG:V_BIG + 4] = _col(f(inp["b_ig"])[0])
    vec[:, V_LAM:V_LAM + 4] = _col(f(inp["lru_lambda"])[0])
    vec[:, V_GLRU:V_GLRU + 4] = _col(f(inp["g_lru_out"])[0])
    vec[:, V_GATTN:V_GATTN + 4] = _col(f(inp["g_attn_out"])[0][pq])
    sk = f(inp["sinks"])[0]
    for g in range(4):
        vec[0:64, V_SINK + g] = sk[g]
        vec[64:128, V_SINK + g] = sk[4 + g]
    sh["vecs"] = vec
    wg = np.zeros((128, 8, 128), np.float32)
    for gi, nm in enumerate(("w_rg", "w_ig")):
        w = f(inp[nm])[0]
        for c in range(4):
            wg[0:64, gi * 4 + c, 0:64] = w[2 * c]
            wg[64:128, gi * 4 + c, 64:128] = w[2 * c + 1]
    sh["wg"] = wg.reshape(128, 1024)
    sh["ident"] = np.eye(128, dtype=np.float32)
    return sh


def core_inputs(sh, x, core, npre):
    b, r = core // 4, core % 4
    ntiles = npre + NOWN
    xs = np.zeros((ntiles * NT, D), np.float32)
    end = (r + 1) * 2048
    start = max(0, end - ntiles * NT)
    xs[ntiles * NT - (end - start):] = x[b, start:end]
    m = dict(sh)
    m["xs"] = xs
    vec = sh["vecs"].copy()
    t0 = ntiles - (end - start) // NT
    for t in range(ntiles):
        vec[:, V_FLAGS + t] = 1.0 if (t - 1) >= t0 else 0.0
    m["vecs"] = vec
    kj = np.arange(128)[:, None]
    qi = np.arange(128)[None, :]
    cur = (kj <= qi).astype(np.float32)
    prev = (kj > qi).astype(np.float32)
    first = prev * (1.0 if r > 0 else 0.0)
    m["masks"] = np.ascontiguousarray(np.stack([cur, prev, first], axis=1)).reshape(128, 384)
    return m


_NC_CACHE = {}


def kernel(**inputs):
    x = np.asarray(inputs["x"], dtype=np.float32)
    sh = prep_shared(inputs)
    npre = 12
    if "full" not in _NC_CACHE:
        _NC_CACHE["full"] = build_program(npre=npre)
    nc = _NC_CACHE["full"]
    in_maps = [core_inputs(sh, x, c, npre) for c in range(8)]
    res = run_bass_kernel_spmd(nc, in_maps, core_ids=list(range(8)))
    outp = np.empty((2, SEQ, D), np.float32)
    for c in range(8):
        b, r = c // 4, c % 4
        outp[b, r * 2048:(r + 1) * 2048] = res.results[c]["out"]
    return outp
```
